# Optimizing a Trainium2 kernel written in Bass

```python
import jax, jax.numpy as jnp
from jax import lax
import numpy as np

D_MODEL = 2048
BATCH = 2
SEQ = 4096
DEPTH = 1

N_HEADS = 16
QK_NOPE_DIM = 128
QK_ROPE_DIM = 64
QK_HEAD_DIM = QK_NOPE_DIM + QK_ROPE_DIM
V_HEAD_DIM = 128
Q_LORA_RANK = 512
KV_LORA_RANK = 512
ROPE_THETA = 10000.0
Q_BLOCK = 128
POOL_WINDOWS = (2, 4, 8, 16)
POOL_GROUPS = 4
POOL_GROUP_DIM = D_MODEL // 8
POOL_WIDTH = POOL_GROUPS * POOL_GROUP_DIM
N_BRANCHES = 2
D_FF = (8 * D_MODEL + 3 * 256 - 1) // (3 * 256) * 256
EPS = 1e-6

IN_SPLITS = (Q_LORA_RANK, KV_LORA_RANK, QK_ROPE_DIM, POOL_WIDTH, N_BRANCHES * D_MODEL)
D_IN = sum(IN_SPLITS)

kernel_name = "hybrid_pool_mla_gated_block"


def rmsnorm(x, g):
    x32 = x.astype(jnp.float32)
    inv = lax.rsqrt(jnp.mean(x32 * x32, axis=-1, keepdims=True) + EPS)
    return (x32 * inv).astype(x.dtype) * g


def apply_rope(t, positions):
    half = QK_ROPE_DIM // 2
    inv_freq = ROPE_THETA ** (-jnp.arange(half, dtype=jnp.float32) / half)
    ang = positions.astype(jnp.float32)[..., None] * inv_freq
    cos = jnp.cos(ang)[:, :, None, :]
    sin = jnp.sin(ang)[:, :, None, :]
    t32 = t.astype(jnp.float32)
    t1, t2 = t32[..., :half], t32[..., half:]
    out = jnp.concatenate([t1 * cos - t2 * sin, t2 * cos + t1 * sin], axis=-1)
    return out.astype(t.dtype)


def causal_multiscale_pool(u):
    B, S, _ = u.shape
    ug = u.reshape(B, S, POOL_GROUPS, POOL_GROUP_DIM)
    cs = jnp.cumsum(ug.astype(jnp.float32), axis=1)
    cs = jnp.pad(cs, ((0, 0), (1, 0), (0, 0), (0, 0)))
    t = jnp.arange(S)
    outs = []
    for g, w in enumerate(POOL_WINDOWS):
        cs_g = cs[:, :, g]
        lo = jnp.maximum(t + 1 - w, 0)
        window_sum = cs_g[:, t + 1] - cs_g[:, lo]
        count = (t + 1 - lo).astype(jnp.float32)
        outs.append(window_sum / count[None, :, None] - ug[:, :, g].astype(jnp.float32))
    return jnp.stack(outs, axis=2).astype(u.dtype)


def causal_block_attention(q, k, v):
    B, S, H, Dq = q.shape
    nb = S // Q_BLOCK
    qb = q.reshape(B, nb, Q_BLOCK, H, Dq).transpose(1, 0, 2, 3, 4)
    kpos = jnp.arange(S)
    scale = QK_HEAD_DIM ** -0.5

    def one_block(args):
        i, qi = args
        s = jnp.einsum('bqhd,bkhd->bhqk', qi, k, preferred_element_type=jnp.float32) * scale
        qpos = i * Q_BLOCK + jnp.arange(Q_BLOCK)
        mask = kpos[None, :] <= qpos[:, None]
        s = jnp.where(mask[None, None], s, -jnp.inf)
        p = jax.nn.softmax(s, axis=-1).astype(v.dtype)
        return jnp.einsum('bhqk,bkhd->bqhd', p, v)

    out = lax.map(one_block, (jnp.arange(nb), qb))
    return out.transpose(1, 0, 2, 3, 4).reshape(B, S, H, v.shape[-1])


def setup_inputs(seed: int = 0) -> dict:
    key = jax.random.key(seed)
    ks = jax.random.split(key, 24)

    def w(k, shape, fan_in):
        return jax.random.normal(k, shape, jnp.float32) * fan_in ** -0.5

    def gain(k, shape):
        return 1.0 + 0.02 * jax.random.normal(k, shape, jnp.float32)

    L = DEPTH
    return {
        "x": jax.random.normal(ks[0], (BATCH, SEQ, D_MODEL), jnp.float32),
        "positions": jnp.broadcast_to(jnp.arange(SEQ, dtype=jnp.int32), (BATCH, SEQ)),
        "attn_norm_g": gain(ks[1], (L, D_MODEL)),
        "w_in": w(ks[2], (L, D_MODEL, D_IN), D_MODEL),
        "b_gate": 0.02 * jax.random.normal(ks[3], (L, N_BRANCHES * D_MODEL), jnp.float32),
        "q_a_norm_g": gain(ks[4], (L, Q_LORA_RANK)),
        "w_q_b": w(ks[5], (L, Q_LORA_RANK, N_HEADS * QK_HEAD_DIM), Q_LORA_RANK),
        "kv_a_norm_g": gain(ks[6], (L, KV_LORA_RANK)),
        "w_kv_b": w(ks[7], (L, KV_LORA_RANK, N_HEADS * (QK_NOPE_DIM + V_HEAD_DIM)), KV_LORA_RANK),
        "q_norm_g": gain(ks[8], (L, QK_HEAD_DIM)),
        "k_norm_g": gain(ks[9], (L, QK_HEAD_DIM)),
        "w_attn_o": w(ks[10], (L, N_HEADS * V_HEAD_DIM, D_MODEL), N_HEADS * V_HEAD_DIM),
        "w_pool_grp": w(ks[11], (L, POOL_GROUPS, POOL_GROUP_DIM, POOL_GROUP_DIM), POOL_GROUP_DIM),
        "pool_scale": gain(ks[12], (L, POOL_GROUPS, POOL_GROUP_DIM)),
        "w_pool_o": w(ks[13], (L, POOL_WIDTH, D_MODEL), POOL_WIDTH),
        "w_out": w(ks[14], (L, D_MODEL, D_MODEL), D_MODEL),
        "ffn_norm_g": gain(ks[15], (L, D_MODEL)),
        "w_ffn_gate": w(ks[16], (L, D_MODEL, D_FF), D_MODEL),
        "w_ffn_up": w(ks[17], (L, D_MODEL, D_FF), D_MODEL),
        "w_ffn_down": w(ks[18], (L, D_FF, D_MODEL), D_FF),
    }


def reference(x, positions, attn_norm_g, w_in, b_gate, q_a_norm_g, w_q_b, kv_a_norm_g, w_kv_b,
              q_norm_g, k_norm_g, w_attn_o, w_pool_grp, pool_scale, w_pool_o, w_out,
              ffn_norm_g, w_ffn_gate, w_ffn_up, w_ffn_down):
    B, S, _ = x.shape
    offsets = list(np.cumsum(IN_SPLITS)[:-1])
    for l in range(DEPTH):
        h = rmsnorm(x, attn_norm_g[l])
        proj = h @ w_in[l]
        c_q, c_kv, k_rope, u_pool, gate_logits = jnp.split(proj, offsets, axis=-1)
        gates = jax.nn.sigmoid(gate_logits + b_gate[l])
        g_pool, g_attn = gates[..., :D_MODEL], gates[..., D_MODEL:]

        pooled = causal_multiscale_pool(u_pool)
        pooled = jnp.einsum('bsgc,gcd->bsgd', pooled, w_pool_grp[l]) * pool_scale[l]
        y_pool = pooled.reshape(B, S, POOL_WIDTH) @ w_pool_o[l]

        q = (rmsnorm(c_q, q_a_norm_g[l]) @ w_q_b[l]).reshape(B, S, N_HEADS, QK_HEAD_DIM)
        kv = (rmsnorm(c_kv, kv_a_norm_g[l]) @ w_kv_b[l]).reshape(B, S, N_HEADS, QK_NOPE_DIM + V_HEAD_DIM)
        k_nope, v = kv[..., :QK_NOPE_DIM], kv[..., QK_NOPE_DIM:]
        k_rope_h = jnp.broadcast_to(k_rope[:, :, None, :], (B, S, N_HEADS, QK_ROPE_DIM))
        k = jnp.concatenate([k_nope, k_rope_h], axis=-1)
        q = rmsnorm(q, q_norm_g[l])
        k = rmsnorm(k, k_norm_g[l])
        q = jnp.concatenate([q[..., :QK_NOPE_DIM], apply_rope(q[..., QK_NOPE_DIM:], positions)], axis=-1)
        k = jnp.concatenate([k[..., :QK_NOPE_DIM], apply_rope(k[..., QK_NOPE_DIM:], positions)], axis=-1)
        attn = causal_block_attention(q, k, v)
        y_attn = attn.reshape(B, S, N_HEADS * V_HEAD_DIM) @ w_attn_o[l]

        mixed = g_pool * y_pool + g_attn * y_attn
        x = x + mixed @ w_out[l]

        h2 = rmsnorm(x, ffn_norm_g[l])
        x = x + (jax.nn.silu(h2 @ w_ffn_gate[l]) * (h2 @ w_ffn_up[l])) @ w_ffn_down[l]
    return x
```

```python
import math
import numpy as np
import concourse.bass as bass
import concourse.mybir as mybir
from concourse.bass_utils import run_bass_kernel_spmd

F32 = mybir.dt.float32
BF16 = mybir.dt.bfloat16
I32 = mybir.dt.int32
AF = mybir.ActivationFunctionType
ALU = mybir.AluOpType

D = 2048
SEQ = 4096
NH = 16
DFF = 5632
NTOK = 1024
EPS = 1e-6
SEM_LIM = 30000


class Op:
    __slots__ = ("eng", "fn", "reads", "writes", "is_dma", "deps", "needs_inc", "sem", "val", "idx", "fence", "semkey")

    def __init__(self, eng, fn, reads, writes, is_dma):
        self.eng = eng
        self.fn = fn
        self.reads = reads
        self.writes = writes
        self.is_dma = is_dma
        self.deps = []
        self.needs_inc = False
        self.sem = None
        self.val = 0


class Prog:
    def __init__(self, nc):
        self.nc = nc
        self.ops = []

    def add(self, eng, fn, reads=(), writes=(), dma=False, fence=None, semkey=None):
        op = Op(eng, fn, tuple(reads), tuple(writes), dma)
        op.fence = fence
        op.semkey = semkey
        op.idx = len(self.ops)
        self.ops.append(op)
        return op

    def build(self, final_wait_engine="sp"):
        nc = self.nc
        ops = self.ops
        last_w = {}
        readers = {}
        fence_of = {}
        kbase = lambda k: k if isinstance(k, str) else k[0]
        for op in ops:
            deps = {}
            if op.fence is not None:
                old_bases, new_base = op.fence
                ob = set(old_bases) | {new_base}
                op.writes = tuple(set(op.writes) | set(k for k in set(last_w) | set(readers) if kbase(k) in ob))
                fence_of[new_base] = op
            for k in op.reads + op.writes:
                if k not in last_w and kbase(k) in fence_of and fence_of[kbase(k)] is not op:
                    last_w[k] = fence_of[kbase(k)]
                    readers.setdefault(k, [])
            rset = set(op.reads)
            wset = set(op.writes)
            for k in rset:
                ispsum = isinstance(k, tuple) and k[0] == "ps"
                lw = last_w.get(k)
                if lw is not None:
                    deps[lw.idx] = max(deps.get(lw.idx, 0), 2)
                if ispsum:
                    for r in readers.get(k, ()):
                        if r.eng != op.eng or r.is_dma:
                            deps[r.idx] = max(deps.get(r.idx, 0), 1)
            for k in wset:
                lw = last_w.get(k)
                if lw is not None:
                    deps[lw.idx] = max(deps.get(lw.idx, 0), 1)
                for r in readers.get(k, ()):
                    deps[r.idx] = max(deps.get(r.idx, 0), 1)
            for di, kind in deps.items():
                d = ops[di]
                if d is op:
                    continue
                if d.eng == op.eng and not d.is_dma and not op.is_dma:
                    if op.eng == "pe":
                        continue
                op.deps.append(d)
                d.needs_inc = True
            for k in rset:
                if k not in wset:
                    readers.setdefault(k, []).append(op)
            for k in wset:
                last_w[k] = op
                readers[k] = []
        sem_pool = {}
        counters = {}
        dma_counts = {}

        def get_sem(name):
            if name not in sem_pool:
                sem_pool[name] = nc.alloc_semaphore(name=name)
            return sem_pool[name]

        for op in ops:
            if op.is_dma:
                key = op.semkey if op.semkey is not None else op.writes[0]
                name = "d_" + "_".join(str(x) for x in (key if isinstance(key, tuple) else (key,)))
                op.sem = get_sem(name)
                dma_counts[name] = dma_counts.get(name, 0) + 16
                op.val = dma_counts[name]
                op.needs_inc = True
            elif op.needs_inc:
                c = counters.get(op.eng, 0)
                op.sem = get_sem(f"e_{op.eng}_{c // SEM_LIM}")
                op.val = c % SEM_LIM + 1
                counters[op.eng] = c + 1
        self.n_sems = len(sem_pool)
        streams = {}
        for op in ops:
            streams.setdefault(op.eng, []).append(op)
        final = [o for o in ops if o.is_dma and any(isinstance(k, tuple) and k[0] == "OUT" for k in o.writes)]
        self.stats = {k: len(v) for k, v in streams.items()}

        def emit(engname, e):
            waited = {}
            for op in streams.get(engname, []):
                for d in op.deps:
                    sid = id(d.sem)
                    if waited.get(sid, 0) >= d.val:
                        continue
                    e.wait_ge(d.sem, d.val)
                    waited[sid] = d.val
                ins = op.fn(e)
                if op.needs_inc:
                    ins.then_inc(op.sem, 16 if op.is_dma else 1)
            if engname == final_wait_engine:
                last = {}
                for o in final:
                    prev = last.get(id(o.sem), (None, 0))[1]
                    last[id(o.sem)] = (o.sem, max(o.val, prev))
                for sem, val in last.values():
                    e.wait_ge(sem, val)

        with nc.Block() as block:
            @block.sync
            def _(e):
                emit("sp", e)

            @block.tensor
            def _(e):
                emit("pe", e)

            @block.scalar
            def _(e):
                emit("act", e)

            @block.vector
            def _(e):
                emit("dve", e)

            @block.gpsimd
            def _(e):
                emit("pool", e)


class Arena:
    def __init__(self, P, nc, base, size):
        self.P = P
        self.nc = nc
        self.free = [(base, size)]
        self.live = {}
        self.dead = []
        self.uid = 0
        self.peak = 0
        self.base = base
        self.dummy = nc.alloc_sbuf_tensor_at("arena_dummy", [128, 8], F32, offset=base - 64)

    def alloc(self, name, shape, dtype, keys=None):
        esz = 2 if dtype == BF16 else 4
        n = 1
        for s in shape[1:]:
            n *= s
        size = (n * esz + 63) // 64 * 64
        off = None
        for i, (o, s) in enumerate(self.free):
            if s >= size:
                off = o
                if s == size:
                    self.free.pop(i)
                else:
                    self.free[i] = (o + size, s - size)
                break
        if off is None:
            raise RuntimeError(f"arena OOM allocating {name} {size} free={self.free}")
        self.peak = max(self.peak, off + size - self.base)
        keys = list(keys) if keys is not None else [name]
        old_keys = []
        newdead = []
        for (do, ds, dk) in self.dead:
            lo = max(do, off)
            hi = min(do + ds, off + size)
            if lo < hi:
                old_keys.extend(dk)
                if do < lo:
                    newdead.append((do, lo - do, dk))
                if hi < do + ds:
                    newdead.append((hi, do + ds - hi, dk))
            else:
                newdead.append((do, ds, dk))
        self.dead = newdead
        if old_keys:
            dummy = self.dummy
            self.P.add("pool", lambda e: e.memset(dummy[:, 0:1], 0.0), [], ["arena_dummy"], fence=(list(dict.fromkeys(old_keys)), name))
        self.uid += 1
        h = self.nc.alloc_sbuf_tensor_at(f"{name}_{self.uid}", list(shape), dtype, offset=off)
        self.live[name] = (off, size, keys)
        return h

    def release(self, name):
        off, size, keys = self.live.pop(name)
        self.dead.append((off, size, keys))
        fl = self.free + [(off, size)]
        fl.sort()
        merged = []
        for o, s in fl:
            if merged and merged[-1][0] + merged[-1][1] == o:
                merged[-1] = (merged[-1][0], merged[-1][1] + s)
            else:
                merged.append((o, s))
        self.free = merged


def col_layout():
    L = {}
    n = 0

    def put(name, w):
        nonlocal n
        L[name] = n
        n += w

    put("g_attn", 16)
    put("g_ffn", 16)
    put("g_qa", 4)
    put("g_kva", 4)
    put("b_gate", 32)
    put("pool_scale", 8)
    put("gq_n", 1)
    put("gq_r", 1)
    put("gq_rs", 1)
    put("gk_n", 1)
    put("gk_r", 1)
    put("gk_rs", 1)
    put("invfreq", 1)
    put("nsign", 1)
    put("eps", 1)
    put("eps192", 1)
    put("negpi", 1)
    put("zero", 1)
    L["_n"] = n
    return L


CL = col_layout()


def build_program(stage=99):
    nc = bass.Bass("TRN2", target_bir_lowering=False)
    dt = lambda name, shape, dty=F32, kind="ExternalInput": nc.dram_tensor(name, list(shape), dty, kind=kind).ap()
    xT_all = dt("xT_all", [D, SEQ])
    xT_own = dt("xT_own", [D, NTOK])
    xT_halo = dt("xT_halo", [D, 128])
    pos_all = dt("pos_all", [64, SEQ], I32)
    pos_own = dt("pos_own", [64, NTOK], I32)
    cols_d = dt("cols", [128, CL["_n"]])
    masks_d = dt("masks", [128, 4 * 128])
    invcnt_d = dt("invcnt", [128, 4 * NTOK])
    w_in = dt("w_in", [D, 6208])
    w_kr = dt("w_kr", [D, 128])
    w_qb2 = dt("w_qb2", [512, NH * 256])
    w_kn = dt("w_kn", [512, NH * 128])
    w_v = dt("w_v", [512, NH * 128])
    w_attn_o = dt("w_attn_o", [D, D])
    w_pool_grp = dt("w_pool_grp", [4 * 256, 256])
    w_pool_o = dt("w_pool_o", [1024, D])
    w_out = dt("w_out", [D, D])
    w_gate = dt("w_ffn_gate", [D, DFF])
    w_up = dt("w_ffn_up", [D, DFF])
    w_down = dt("w_ffn_down", [DFF, D])
    outT = dt("outT", [D, NTOK], F32, kind="ExternalOutput")
    xmid_d = dt("xmid_scr", [D, NTOK], F32, kind="Internal")
    dbg = None
    if stage < 99:
        dbg = dt("dbg", [128, 8192], F32, kind="ExternalOutput")

    P = Prog(nc)
    A = Arena(P, nc, 16704, 212000)
    banks = [nc.alloc_psum_tensor(f"bank{i}", [128, 512], F32) for i in range(8)]
    PS = lambda b: ("ps", b)

    rot = {"i": 0, "set": [0, 1, 2, 3, 4, 5, 6, 7]}

    def nb():
        s = rot["set"]
        b = s[rot["i"] % len(s)]
        rot["i"] += 1
        return b

    def wdma(dst_ap, src_ap, key, reads=()):
        P.add("pool", lambda e: e.dma_start(out=dst_ap, in_=src_ap), list(reads), [key], dma=True)

    def ldma(dst_ap, src_ap, key, reads=()):
        P.add("sp", lambda e: e.dma_start(out=dst_ap, in_=src_ap), list(reads), [key], dma=True)

    def mm(bank, out_ap, lhsT, rhs, start, stop, reads, kind=0):
        if kind == 0:
            fn = lambda e: e.matmul(out_ap, lhsT, rhs, start=start, stop=stop)
        elif kind == 1:
            fn = lambda e: e.matmul(out_ap, lhsT, rhs, start=start, stop=stop)
        elif kind == 2:
            fn = lambda e: e.matmul(out_ap, lhsT, rhs, start=start, stop=stop)
        elif kind == 3:
            fn = lambda e: e.matmul(out_ap, lhsT, rhs, start=start, stop=stop)
        elif kind == 4:
            fn = lambda e: e.matmul(out_ap, lhsT, rhs, start=start, stop=stop)
        elif kind == 5:
            fn = lambda e: e.matmul(out_ap, lhsT, rhs, start=start, stop=stop)
        else:
            fn = lambda e: e.matmul(out_ap, lhsT, rhs, start=start, stop=stop)
        P.add("pe", fn, list(reads), [PS(bank)])

    def wview(w_dram, c0, c1):
        return w_dram[:, c0:c1].rearrange("(kc p) n -> p kc n", p=128)

    cols = A.alloc("cols", [128, CL["_n"]], F32)
    ldma(cols[:, :], cols_d, "cols")
    C = lambda name, i=0, p0=0, p1=128: cols[p0:p1, CL[name] + i:CL[name] + i + 1]
    ones_bf = A.alloc("ones_bf", [128, 128], BF16)
    P.add("pool", lambda e: e.memset(ones_bf[:, :], 1.0), [], ["ones_bf"])
    neghalf = A.alloc("neghalf", [128, 512], F32)
    P.add("pool", lambda e: e.memset(neghalf[:, :], -0.5), [], ["neghalf"])
    maskb = A.alloc("maskb", [128, 512], BF16)
    wdma(maskb[:, :], masks_d, "maskb")

    def rstd_from_psum(bank, n, scale, eps_col, out_ap, out_key, tmp, tmp_key, np_=128):
        P.add("act", lambda e: e.activation(out=tmp[0:np_, 0:n], in_=banks[bank][0:np_, 0:n], func=AF.Ln,
                                            bias=C(eps_col, 0, 0, np_), scale=scale),
              [PS(bank), "cols"], [tmp_key])
        P.add("act", lambda e: e.activation(out=out_ap, in_=tmp[0:np_, 0:n], func=AF.Exp, scale=-0.5),
              [tmp_key], [out_key])

    RT = {}

    def rope_setup(tagname):
        RT["pos"] = [A.alloc(f"rt_pos{i}", [64, 512], I32) for i in range(2)]
        for nm in ("rt_ang", "rt_kf", "rt_r1", "rt_r2"):
            RT[nm] = A.alloc(nm, [64, 512], F32)
        RT["gs"] = A.alloc("rt_gs", [64, 2], F32)

    def rope_release():
        for nm in ("rt_pos0", "rt_pos1", "rt_ang", "rt_kf", "rt_r1", "rt_r2", "rt_gs"):
            A.release(nm)

    def rope_groups(pos_d, c, gr, grs, cosg, sing, name):
        TWO_PI = 2.0 * math.pi
        pi_, pk = RT["pos"][c % 2], f"rt_pos{c % 2}"
        ang, kf, r1, r2, gs = RT["rt_ang"], RT["rt_kf"], RT["rt_r1"], RT["rt_r2"], RT["gs"]
        c0 = c * 512

        def fold_clamp(r, rk_):
            P.add("dve", lambda e: e.tensor_single_scalar(out=kf[:, :], in_=r[:, :], scalar=math.pi, op=ALU.is_gt), [rk_], ["rt_kf"])
            P.add("dve", lambda e: e.scalar_tensor_tensor(out=r[:, :], in0=kf[:, :], scalar=-TWO_PI, in1=r[:, :], op0=ALU.mult, op1=ALU.add),
                  ["rt_kf", rk_], [rk_])
            P.add("dve", lambda e: e.tensor_scalar(out=r[:, :], in0=r[:, :], scalar1=math.pi, scalar2=-math.pi, op0=ALU.min, op1=ALU.max),
                  [rk_], [rk_])

        def g_sin():
            ldma(pi_[:, :], pos_d[:, c0:c0 + 512], pk)
            P.add("dve", lambda e: e.tensor_tensor(out=gs[:, 0:1], in0=C(grs, 0, 0, 64), in1=C("nsign", 0, 0, 64), op=ALU.mult), ["cols"], ["rt_gs"])
            P.add("dve", lambda e: e.tensor_copy(out=ang[:, :], in_=pi_[:, :]), [pk], ["rt_ang"])
            P.add("dve", lambda e: e.tensor_scalar(out=ang[:, :], in0=ang[:, :], scalar1=C("invfreq", 0, 0, 64), scalar2=None, op0=ALU.mult),
                  ["rt_ang", "cols"], ["rt_ang"])
            P.add("act", lambda e: e.activation(out=kf[:, :], in_=ang[:, :], func=AF.Copy, scale=1.0 / TWO_PI), ["rt_ang"], ["rt_kf"])
            P.add("dve", lambda e: e.tensor_copy(out=pi_[:, :], in_=kf[:, :]), ["rt_kf"], [pk])
            P.add("dve", lambda e: e.tensor_copy(out=kf[:, :], in_=pi_[:, :]), [pk], ["rt_kf"])
            P.add("dve", lambda e: e.scalar_tensor_tensor(out=r1[:, :], in0=kf[:, :], scalar=-TWO_PI, in1=ang[:, :], op0=ALU.mult, op1=ALU.add),
                  ["rt_kf", "rt_ang"], ["rt_r1"])
            fold_clamp(r1, "rt_r1")
            P.add("dve", lambda e: e.tensor_scalar(out=r2[:, :], in0=r1[:, :], scalar1=0.5 * math.pi, scalar2=None, op0=ALU.add), ["rt_r1"], ["rt_r2"])
            P.add("act", lambda e: e.activation(out=r1[:, :], in_=r1[:, :], func=AF.Sin), ["rt_r1"], ["rt_r1"])
            P.add("act", lambda e: e.activation(out=sing[:, c0:c0 + 512], in_=r1[:, :], func=AF.Copy, scale=gs[:, 0:1]),
                  ["rt_r1", "rt_gs"], [(name + "_sin", c)])

        def g_cos():
            fold_clamp(r2, "rt_r2")
            P.add("act", lambda e: e.activation(out=r2[:, :], in_=r2[:, :], func=AF.Sin), ["rt_r2"], ["rt_r2"])
            P.add("act", lambda e: e.activation(out=cosg[:, c0:c0 + 512], in_=r2[:, :], func=AF.Copy, scale=C(gr, 0, 0, 64)),
                  ["rt_r2", "cols"], [(name + "_cos", c)])

        return [g_sin, g_cos]

    def dbg_out(src_ap, np_, n, reads, col0=0):
        P.add("sp", lambda e: e.dma_start(out=dbg[0:np_, col0:col0 + n], in_=src_ap), list(reads), [("OUT", "dbg", col0)], dma=True)

    ckvn = A.alloc("ckvn", [128, 4, SEQ], BF16)
    krr = A.alloc("krr", [64, SEQ], F32)
    sqr = A.alloc("sqr", [64, SEQ], BF16)
    wkv = A.alloc("wkv", [128, 16, 640], BF16)
    wdma(wkv[:, :, 0:576], wview(w_in, 512, 1088), ("wkv", 0))
    wdma(wkv[:, :, 576:640], wview(w_kr, 64, 128), ("wkv", 1))
    WKV = [("wkv", 0), ("wkv", 1)]
    cosK = A.alloc("ropeK_cos", [64, SEQ], F32)
    sinK = A.alloc("ropeK_sin", [64, SEQ], F32)
    rope_setup("K")
    rgroups = []
    for c in range(8):
        rgroups.extend(rope_groups(pos_all, c, "gk_r", "gk_rs", cosK, sinK, "ropeK"))

    NR = 6
    xt = [A.alloc(f"xt{i}", [128, 512], F32) for i in range(NR)]
    sq = [A.alloc(f"sq{i}", [128, 512], BF16) for i in range(NR)]
    xg = [A.alloc(f"xg{i}", [128, 16, 512], BF16) for i in range(2)]
    ckv = A.alloc("ckv", [128, 4, 512], F32)
    sq2 = A.alloc("sqc", [128, 4, 512], BF16)
    rsx = A.alloc("rsx", [128, 512], F32)
    rskv = A.alloc("rskv", [128, 512], F32)
    tmpa = A.alloc("tmpa", [128, 512], F32)
    tmpa2a = A.alloc("tmpa2a", [128, 512], F32)
    krt = [A.alloc(f"krt{i}", [64, 512], F32) for i in range(2)]
    krs = [A.alloc(f"krs{i}", [64, 512], F32) for i in range(2)]
    rc = [0]

    def nf_load(src_d, t0, n, xgbuf, xgkey, gname, fc):
        s = rc[0] % NR
        rc[0] += 1
        ldma(xt[s][:, 0:n], src_d[fc * 128:(fc + 1) * 128, t0:t0 + n], f"xt{s}")
        P.add("pool", lambda e, s=s: e.tensor_tensor(out=sq[s][:, 0:n], in0=xt[s][:, 0:n], in1=xt[s][:, 0:n], op=ALU.mult),
              [f"xt{s}"], [f"sq{s}"])
        P.add("act", lambda e, s=s: e.activation(out=xgbuf[:, fc, 0:n], in_=xt[s][:, 0:n], func=AF.Copy, scale=C(gname, fc)),
              [f"xt{s}", "cols"], [xgkey])
        return s

    def nf_mm(s, n, fc, bssq):
        mm(bssq, banks[bssq][:, 0:n], ones_bf[:, :], sq[s][:, 0:n], fc == 0, fc == 15, ["ones_bf", f"sq{s}"], kind=1)

    def nf_chunk(src_d, t0, n, xgbuf, xgkey, gname, fc, bssq):
        s = nf_load(src_d, t0, n, xgbuf, xgkey, gname, fc)
        nf_mm(s, n, fc, bssq)

    bs_next = nb()
    for fc in range(16):
        nf_chunk(xT_all, 0, 512, xg[0], "xg0", "g_attn", fc, bs_next)
    rgroups.pop(0)()
    rgroups.pop(0)()
    pending_tail = []
    for tt in range(8):
        xb = xg[tt % 2]
        xk = f"xg{tt % 2}"
        bssq = bs_next
        rstd_from_psum(bssq, 512, 1.0 / D, "eps", rsx[:, :], "rsx", tmpa, "tmpa")
        T = slice(tt * 512, (tt + 1) * 512)
        k2 = tt % 2
        if tt + 1 < 8:
            bs_next = nb()
        nxt = list(range(16))
        pend_mm = []
        for m in range(6):
            b = nb()
            if m < 4:
                for kc in range(16):
                    mm(b, banks[b][:, :], wkv[:, kc, m * 128:(m + 1) * 128], xb[:, kc, :], kc == 0, kc == 15, WKV + [xk], kind=2)
                if m == 0 and pending_tail:
                    pending_tail.pop(0)()
                P.add("dve", lambda e, b=b, m=m: e.tensor_tensor(out=ckv[:, m, :], in0=banks[b][:, :], in1=rsx[:, :], op=ALU.mult),
                      [PS(b), "rsx"], [("ckv", m)])
                P.add("dve", lambda e, m=m: e.tensor_tensor(out=sq2[:, m, :], in0=ckv[:, m, :], in1=ckv[:, m, :], op=ALU.mult),
                      [("ckv", m)], [("sqc", m)])
            else:
                c0 = 512 + (m - 4) * 64
                for kc in range(16):
                    mm(b, banks[b][0:64, :], wkv[:, kc, c0:c0 + 64], xb[:, kc, :], kc == 0, kc == 15, WKV + [xk], kind=2)
                dst, dk = (krt[k2], f"krt{k2}") if m == 4 else (krs[k2], f"krs{k2}")
                P.add("dve", lambda e, b=b, dst=dst: e.tensor_tensor(out=dst[:, :], in0=banks[b][0:64, :], in1=rsx[0:64, :], op=ALU.mult),
                      [PS(b), "rsx"], [dk])
                if m == 4:
                    P.add("dve", lambda e, T=T, dst=dst: e.tensor_tensor(out=sqr[:, T], in0=dst[:, :], in1=dst[:, :], op=ALU.mult),
                          [dk], [("sqr", tt)])
            for (s_, fc_) in pend_mm:
                nf_mm(s_, 512, fc_, bs_next)
            pend_mm = []
            if tt + 1 < 8:
                for _ in range(3 if m < 5 else 1):
                    if nxt:
                        fc = nxt.pop(0)
                        s_ = nf_load(xT_all, (tt + 1) * 512, 512, xg[(tt + 1) % 2], f"xg{(tt + 1) % 2}", "g_attn", fc)
                        pend_mm.append((s_, fc))
            if rgroups and m in (1, 3):
                rgroups.pop(0)()
        for (s_, fc_) in pend_mm:
            nf_mm(s_, 512, fc_, bs_next)

        def tail(tt=tt, T=T, k2=k2):
            b = nb()
            for m in range(4):
                mm(b, banks[b][:, :], ones_bf[:, :], sq2[:, m, :], m == 0, m == 3, ["ones_bf", ("sqc", m)], kind=1)
            rstd_from_psum(b, 512, 1.0 / 512, "eps", rskv[:, :], "rskv", tmpa2a, "tmpa2a")
            for m in range(4):
                P.add("dve", lambda e, m=m: e.scalar_tensor_tensor(out=ckvn[:, m, T], in0=ckv[:, m, :], scalar=C("g_kva", m),
                                                                   in1=rskv[:, :], op0=ALU.mult, op1=ALU.mult),
                      [("ckv", m), "cols", "rskv"], [("ckvn", tt)])
            P.add("pool", lambda e: e.tensor_tensor(out=krt[k2][:, :], in0=krt[k2][:, :], in1=cosK[:, T], op=ALU.mult),
                  [f"krt{k2}", ("ropeK_cos", tt)], [f"krt{k2}"])
            P.add("pool", lambda e: e.tensor_tensor(out=krs[k2][:, :], in0=krs[k2][:, :], in1=sinK[:, T], op=ALU.mult),
                  [f"krs{k2}", ("ropeK_sin", tt)], [f"krs{k2}"])
            P.add("pool", lambda e: e.tensor_tensor(out=krr[:, T], in0=krt[k2][:, :], in1=krs[k2][:, :], op=ALU.add),
                  [f"krt{k2}", f"krs{k2}"], [("krr", tt)])
        pending_tail.append(tail)
    while pending_tail:
        pending_tail.pop(0)()
    CKVN = [("ckvn", t) for t in range(8)]
    KRR = [("krr", t) for t in range(8)]
    SQR = [("sqr", t) for t in range(8)]
    if stage == 1:
        d1 = A.alloc("d1", [128, 2048], F32)
        P.add("dve", lambda e: e.tensor_copy(out=d1[:, :], in_=ckvn[:, 0, 0:2048]), CKVN, ["d1"])
        dbg_out(d1[:, :], 128, 2048, ["d1"], 0)
        dbg_out(krr[:, 0:2048], 64, 2048, KRR, 2048)
        dbg_out(cosK[:, 0:2048], 64, 2048, [("ropeK_cos", i) for i in range(4)], 4096)
        dbg_out(sinK[:, 0:2048], 64, 2048, [("ropeK_sin", i) for i in range(4)], 6144)
        P.build()
        return nc, P, A
    for nme in ("wkv", "ckv", "sqc", "rskv", "krt0", "krt1", "krs0", "krs1", "ropeK_cos", "ropeK_sin"):
        A.release(nme)

    cqn = A.alloc("cqn", [128, 4, NTOK], BF16)
    wq = A.alloc("wq", [128, 16, 512], BF16)
    wdma(wq[:, :, :], wview(w_in, 0, 512), "wq")
    cosQ = A.alloc("ropeQ_cos", [64, NTOK], F32)
    sinQ = A.alloc("ropeQ_sin", [64, NTOK], F32)
    for c in range(2):
        for g_ in rope_groups(pos_own, c, "gq_r", "gq_rs", cosQ, sinQ, "ropeQ"):
            g_()
    cq = A.alloc("cq", [128, 4, 512], F32)
    sq2b = A.alloc("sq2b", [128, 4, 512], BF16)
    rsq = A.alloc("rsq", [128, 512], F32)
    bs_next = nb()
    for fc in range(16):
        nf_chunk(xT_own, 0, 512, xg[0], "xg0", "g_attn", fc, bs_next)
    for hh in range(2):
        xb = xg[hh]
        xk = f"xg{hh}"
        bssq = bs_next
        rstd_from_psum(bssq, 512, 1.0 / D, "eps", rsx[:, :], "rsx", tmpa, "tmpa")
        T = slice(hh * 512, (hh + 1) * 512)
        if hh == 0:
            bs_next = nb()
        nxt = list(range(16))
        for m in range(4):
            b = nb()
            for kc in range(16):
                mm(b, banks[b][:, :], wq[:, kc, m * 128:(m + 1) * 128], xb[:, kc, :], kc == 0, kc == 15, ["wq", xk], kind=2)
            P.add("dve", lambda e, b=b, m=m: e.tensor_tensor(out=cq[:, m, :], in0=banks[b][:, :], in1=rsx[:, :], op=ALU.mult),
                  [PS(b), "rsx"], [("cq", m)])
            P.add("act", lambda e, m=m: e.activation(out=sq2b[:, m, :], in_=cq[:, m, :], func=AF.Square),
                  [("cq", m)], [("sq2b", m)])
            if hh == 0:
                for _ in range(4):
                    fc = nxt.pop(0)
                    nf_chunk(xT_own, 512, 512, xg[1], "xg1", "g_attn", fc, bs_next)
        b = nb()
        for m in range(4):
            mm(b, banks[b][:, :], ones_bf[:, :], sq2b[:, m, :], m == 0, m == 3, ["ones_bf", ("sq2b", m)], kind=1)
        rstd_from_psum(b, 512, 1.0 / 512, "eps", rsq[:, :], "rsq", tmpa2a, "tmpa2a")
        for m in range(4):
            P.add("dve", lambda e, m=m, T=T: e.scalar_tensor_tensor(out=cqn[:, m, T], in0=cq[:, m, :], scalar=C("g_qa", m),
                                                                     in1=rsq[:, :], op0=ALU.mult, op1=ALU.mult),
                  [("cq", m), "cols", "rsq"], [("cqn", hh)])
    CQN = [("cqn", 0), ("cqn", 1)]
    if stage == 2:
        d1 = A.alloc("d1", [128, 1024], F32)
        P.add("dve", lambda e: e.tensor_copy(out=d1[:, :], in_=cqn[:, 0, :]), CQN, ["d1"])
        dbg_out(d1[:, :], 128, 1024, ["d1"], 0)
        dbg_out(cosQ[:, :], 64, 1024, [("ropeQ_cos", 0), ("ropeQ_cos", 1)], 1024)
        P.build()
        return nc, P, A
    rope_release()
    for nme in ("wq", "cq", "sq2b", "rsq", "rsx", "tmpa2a") + tuple(f"xt{i}" for i in range(NR)) + tuple(f"sq{i}" for i in range(NR)) + ("xg0", "xg1"):
        A.release(nme)

    oall = A.alloc("oall", [128, NH, NTOK], BF16)
    Krb = A.alloc("Krb", [128, SEQ], BF16)
    P.add("pool", lambda e: e.memset(Krb[64:128, :], 0.0), [], [("Krb", "pad")])
    ssqr_col = A.alloc("ssqr_col", [128, 32], F32)
    COLB = 3
    for tt in range(8):
        T = slice(tt * 512, (tt + 1) * 512)
        P.add("act", lambda e, T=T: e.activation(out=Krb[0:64, T], in_=krr[:, T], func=AF.Copy), [("krr", tt)], [("Krb", tt)])
    for kc in range(32):
        mm(COLB, banks[COLB][:, kc:kc + 1], sqr[:, kc * 128:(kc + 1) * 128], ones_bf[0:64, 0:1], True, True, [("sqr", kc // 4), "ones_bf"])
    P.add("dve", lambda e: e.tensor_copy(out=ssqr_col[:, :], in_=banks[COLB][:, 0:32]), [PS(COLB)], ["ssqr_col"])
    P.add("pool", lambda e: e.memset(A.dummy[:, 1:2], 0.0), [("krr", t) for t in range(8)] + [("sqr", t) for t in range(8)], ["krr_done"])
    A.release("krr")
    A.release("sqr")
    Kn = A.alloc("Kn", [128, SEQ], BF16)
    V4 = A.alloc("V4", [128, 32, 512], BF16)
    Qn = A.alloc("Qn", [128, NTOK], BF16)
    Qr = A.alloc("Qr", [128, NTOK], BF16)
    P.add("pool", lambda e: e.memset(Qr[64:128, :], 0.0), [], [("Qr", "pad")])
    wkn = [A.alloc(f"wkn{i}", [128, 4, 128], BF16) for i in range(2)]
    wv4 = A.alloc("wv4", [128, 4, 512], BF16)
    wqb = [A.alloc(f"wqb{i}", [128, 4, 256], BF16) for i in range(2)]
    NPT = 6
    PT = [A.alloc(f"PT{i}", [128, 512], BF16) for i in range(NPT)]
    sqk = [A.alloc(f"sqk{i}", [128, 512], BF16) for i in range(3)]
    sqq = A.alloc("sqq", [64, 512], BF16)
    rk = [A.alloc(f"rk{i}", [128, 512], F32) for i in range(1)]
    tmpc = [A.alloc(f"tmpc{i}", [128, 512], F32) for i in range(1)]
    t1 = A.alloc("t1", [64, 512], F32)
    t2 = A.alloc("t2", [64, 512], F32)
    rec = A.alloc("rec", [128, 512], F32)
    tot = A.alloc("tot", [128, 32], F32)
    scl = [A.alloc(f"scl{i}", [128, 32], F32) for i in range(2)]
    SC = math.sqrt(192.0)
    pti = 0
    nheads = NH if stage != 3 else 2
    GEN = [0, 1, 2, 4, 5, 6, 7]
    SB = [0, 1, 2]
    OB = [4, 5]
    DB = [6, 7]

    def load_head(hx):
        wdma(wkn[hx % 2][:, :, :], wview(w_kn, hx * 128, (hx + 1) * 128), f"wkn{hx % 2}")
        wdma(wqb[hx % 2][:, :, :], wview(w_qb2, hx * 256, (hx + 1) * 256), f"wqb{hx % 2}")

    D1 = {}

    def d1_front_setup():
        A.release("ckvn")
        A.release("cqn")
        D1["xgo"] = A.alloc("xgo", [128, 16, NTOK], BF16)
        D1["xgh"] = A.alloc("xgh", [128, 16, 128], BF16)
        D1["rso"] = A.alloc("rso", [128, NTOK], F32)
        D1["rsh"] = A.alloc("rsh", [128, 128], F32)
        D1["tmpa2"] = A.alloc("tmpa2", [128, 512], F32)
        D1["xtd"] = [A.alloc(f"xt{i}", [128, 512], F32) for i in range(4)]
        D1["sqd"] = [A.alloc(f"sq{i}", [128, 512], BF16) for i in range(4)]
        xgo_, xgh_, rso_, rsh_, tmpa2_, xtd_, sqd_ = (D1[k] for k in ("xgo", "xgh", "rso", "rsh", "tmpa2", "xtd", "sqd"))
        out = []
        cnt = [0]

        def chunk(src_d, t0, n, dst, dkey, c0, fc, last_fn):
            def f():
                s_ = cnt[0] % 4
                cnt[0] += 1
                ldma(xtd_[s_][:, 0:n], src_d[fc * 128:(fc + 1) * 128, t0:t0 + n], f"xt{s_}")
                P.add("dve", lambda e: e.tensor_tensor(out=sqd_[s_][:, 0:n], in0=xtd_[s_][:, 0:n], in1=xtd_[s_][:, 0:n], op=ALU.mult),
                      [f"xt{s_}"], [f"sq{s_}"])
                P.add("act", lambda e: e.activation(out=dst[:, fc, c0:c0 + n], in_=xtd_[s_][:, 0:n], func=AF.Copy, scale=C("g_attn", fc)),
                      [f"xt{s_}", "cols"], [dkey])
                mm(COLB, banks[COLB][:, 0:n], ones_bf[:, :], sqd_[s_][:, 0:n], fc == 0, fc == 15, ["ones_bf", f"sq{s_}"], kind=1)
                if fc == 15:
                    last_fn()
            return f

        for hh in range(2):
            lf = (lambda hh=hh: rstd_from_psum(COLB, 512, 1.0 / D, "eps", rso_[:, hh * 512:(hh + 1) * 512], ("rso", hh), tmpa2_, "tmpa2"))
            for fc in range(16):
                out.append(chunk(xT_own, hh * 512, 512, xgo_, ("xgo", hh), hh * 512, fc, lf))
        lf = (lambda: rstd_from_psum(COLB, 128, 1.0 / D, "eps", rsh_[:, :], "rsh", tmpa2_, "tmpa2"))
        for fc in range(16):
            out.append(chunk(xT_halo, 0, 128, xgh_, "xgh", 0, fc, lf))
        return out

    for h in range(nheads):
        wk_, wkk = wkn[h % 2], f"wkn{h % 2}"
        wq_, wqk = wqb[h % 2], f"wqb{h % 2}"
        sc_, sck = scl[h % 2], f"scl{h % 2}"
        if h == 0:
            load_head(0)
        if h + 1 < nheads:
            load_head(h + 1)
        rot["set"] = GEN
        rot["i"] = 0
        for hh in range(2):
            T = slice(hh * 512, (hh + 1) * 512)
            bn, br, bs_ = nb(), nb(), nb()
            for kc in range(4):
                mm(bn, banks[bn][:, :], wq_[:, kc, 0:128], cqn[:, kc, T], kc == 0, kc == 3, [wqk, ("cqn", hh)])
            for kc in range(4):
                mm(br, banks[br][0:64, :], wq_[:, kc, 128:192], cqn[:, kc, T], kc == 0, kc == 3, [wqk, ("cqn", hh)])
            for kc in range(4):
                mm(bs_, banks[bs_][0:64, :], wq_[:, kc, 192:256], cqn[:, kc, T], kc == 0, kc == 3, [wqk, ("cqn", hh)])
            P.add("act", lambda e, bn=bn: e.activation(out=sqk[0][:, :], in_=banks[bn][:, :], func=AF.Square), [PS(bn)], ["sqk0"])
            P.add("act", lambda e, br=br: e.activation(out=sqq[:, :], in_=banks[br][0:64, :], func=AF.Square), [PS(br)], ["sqq"])
            bq = nb()
            mm(bq, banks[bq][:, :], ones_bf[:, :], sqk[0][:, :], True, False, ["ones_bf", "sqk0"])
            mm(bq, banks[bq][:, :], ones_bf[0:64, :], sqq[:, :], False, True, ["ones_bf", "sqq"])
            rstd_from_psum(bq, 512, 1.0, "eps192", rk[0][:, :], "rk0", tmpc[0], "tmpc0")
            P.add("dve", lambda e, bn=bn, T=T: e.scalar_tensor_tensor(out=Qn[:, T], in0=banks[bn][:, :], scalar=C("gq_n"),
                                                                      in1=rk[0][:, :], op0=ALU.mult, op1=ALU.mult),
                  [PS(bn), "cols", "rk0"], [("Qn", hh)])
            P.add("dve", lambda e, br=br, T=T: e.tensor_tensor(out=t1[:, :], in0=banks[br][0:64, :], in1=cosQ[:, T], op=ALU.mult),
                  [PS(br), ("ropeQ_cos", hh)], ["t1"])
            P.add("dve", lambda e, bs_=bs_, T=T: e.tensor_tensor(out=t2[:, :], in0=banks[bs_][0:64, :], in1=sinQ[:, T], op=ALU.mult),
                  [PS(bs_), ("ropeQ_sin", hh)], ["t2"])
            P.add("dve", lambda e: e.tensor_tensor(out=t1[:, :], in0=t1[:, :], in1=t2[:, :], op=ALU.add), ["t1", "t2"], ["t1"])
            P.add("dve", lambda e, T=T: e.tensor_tensor(out=Qr[0:64, T], in0=t1[:, :], in1=rk[0][0:64, :], op=ALU.mult),
                  ["t1", "rk0"], [("Qr", hh)])
        if h % 4 == 0:
            wdma(wv4[:, :, :], wview(w_v, h * 128, (h + 4) * 128), "wv4")
            for tc in range(32):
                bv = nb()
                for kc in range(4):
                    mm(bv, banks[bv][:, :], ckvn[:, kc, tc * 128:(tc + 1) * 128], wv4[:, kc, :], kc == 0, kc == 3, ["wv4", ("ckvn", tc // 4)])
                if tc % 2 == 0:
                    P.add("act", lambda e, bv=bv, tc=tc: e.activation(out=V4[:, tc, :], in_=banks[bv][:, :], func=AF.Copy), [PS(bv)], [("V4", tc // 4)])
                else:
                    P.add("dve", lambda e, bv=bv, tc=tc: e.tensor_copy(out=V4[:, tc, :], in_=banks[bv][:, :]), [PS(bv)], [("V4", tc // 4)])
        hl = h % 4
        def k_mm(tt):
            bk = nb()
            T = slice(tt * 512, (tt + 1) * 512)
            for kc in range(4):
                mm(bk, banks[bk][:, :], wk_[:, kc, :], ckvn[:, kc, T], kc == 0, kc == 3, [wkk, ("ckvn", tt)])
            i3 = tt % 3
            P.add("act", lambda e, bk=bk, i3=i3: e.activation(out=sqk[i3][:, :], in_=banks[bk][:, :], func=AF.Square), [PS(bk)], [f"sqk{i3}"])
            P.add("dve", lambda e, bk=bk, T=T: e.tensor_scalar(out=Kn[:, T], in0=banks[bk][:, :], scalar1=C("gk_n"), scalar2=None, op0=ALU.mult),
                  [PS(bk), "cols"], [("Kn", tt)])

        def k_col(tt):
            i3 = tt % 3
            for c in range(4):
                kc = tt * 4 + c
                mm(COLB, banks[COLB][:, kc:kc + 1], sqk[i3][:, c * 128:(c + 1) * 128], ones_bf[:, 0:1], True, True, [f"sqk{i3}", "ones_bf"])

        k_mm(0)
        for tt in range(8):
            if tt + 1 < 8:
                k_mm(tt + 1)
            k_col(tt)
        P.add("dve", lambda e: e.tensor_tensor(out=tot[:, :], in0=banks[COLB][:, 0:32], in1=ssqr_col[:, :], op=ALU.add), [PS(COLB), "ssqr_col"], ["tot"])
        P.add("act", lambda e: e.activation(out=tot[:, :], in_=tot[:, :], func=AF.Ln, bias=C("eps192"), scale=1.0), ["tot", "cols"], ["tot"])
        P.add("act", lambda e: e.activation(out=tot[:, :], in_=tot[:, :], func=AF.Exp, scale=-0.5), ["tot"], ["tot"])
        P.add("dve", lambda e, sc_=sc_: e.tensor_scalar(out=sc_[:, :], in0=tot[:, :], scalar1=SC, scalar2=None, op0=ALU.mult), ["tot"], [sck])
        extra = []
        if h == NH - 1 and stage > 3:
            extra = d1_front_setup()
        items = []
        for t in range(8):
            pieces = [(128 * t, 512, 0), (512, 1024, 1)] if t < 4 else [(128 * t, 1024, 1)]
            for c in range(4):
                for (q0, q1, half) in pieces:
                    items.append((t, c, q0, q1, half))
        rest = items[2:]
        big = [it_ for it_ in rest if it_[3] - it_[2] >= 384]
        small = [it_ for it_ in rest if it_[3] - it_[2] < 384]
        items = items[:2]
        while big or small:
            for _ in range(2):
                if big:
                    items.append(big.pop(0))
            if small:
                items.append(small.pop(0))
        first_idx = {}
        last_idx = {}
        for ii, it_ in enumerate(items):
            first_idx.setdefault(it_[4], ii)
            last_idx[it_[4]] = ii
        sbank = {}

        def emit_S(i):
            t, c, q0, q1, half = items[i]
            kc = 4 * t + c
            KS = slice(kc * 128, (kc + 1) * 128)
            n = q1 - q0
            bS = SB[i % 3]
            sbank[i] = bS
            mm(bS, banks[bS][:, 0:n], Kn[:, KS], Qn[:, q0:q1], True, False, [("Kn", t), ("Qn", half)], kind=4)
            mm(bS, banks[bS][:, 0:n], Krb[:, KS], Qr[:, q0:q1], False, True, [("Krb", t), ("Krb", "pad"), ("Qr", half), ("Qr", "pad")], kind=4)

        LA = 2
        for i in range(min(LA, len(items))):
            emit_S(i)
        for i in range(len(items)):
            if i + LA < len(items):
                emit_S(i + LA)
            t, c, q0, q1, half = items[i]
            kc = 4 * t + c
            n = q1 - q0
            bS = sbank[i]
            pt = PT[pti % NPT]
            pk = f"PT{pti % NPT}"
            pti += 1
            P.add("act", lambda e, bS=bS, pt=pt, n=n, kc=kc, sc_=sc_: e.activation(out=pt[:, 0:n], in_=banks[bS][:, 0:n], func=AF.Exp,
                                                                                 scale=sc_[:, kc:kc + 1]),
                  [PS(bS), sck], [pk])
            if q0 == 128 * t:
                P.add("dve", lambda e, pt=pt, c=c: e.tensor_tensor(out=pt[:, 0:128], in0=pt[:, 0:128],
                                                                  in1=maskb[:, c * 128:(c + 1) * 128], op=ALU.mult),
                      [pk, "maskb"], [pk])
            first = (first_idx[half] == i)
            last = (last_idx[half] == i)
            ob, db = OB[half], DB[half]
            o0 = q0 - 512 * half
            mm(ob, banks[ob][:, o0:o0 + n], V4[:, kc, hl * 128:(hl + 1) * 128], pt[:, 0:n], first, last, [("V4", kc // 4), pk], kind=5)
            mm(db, banks[db][:, o0:o0 + n], ones_bf[:, :], pt[:, 0:n], first, last, ["ones_bf", pk], kind=6)
            if extra:
                extra.pop(0)()
        while extra:
            extra.pop(0)()
        for half in range(2):
            ob, db = OB[half], DB[half]
            P.add("act", lambda e, db=db: e.activation(out=rec[:, :], in_=banks[db][:, :], func=AF.Ln), [PS(db)], ["rec"])
            P.add("act", lambda e: e.activation(out=rec[:, :], in_=rec[:, :], func=AF.Exp, scale=-1.0), ["rec"], ["rec"])
            P.add("dve", lambda e, ob=ob, half=half, h=h: e.tensor_tensor(out=oall[:, h, half * 512:(half + 1) * 512],
                                                                           in0=banks[ob][:, :], in1=rec[:, :], op=ALU.mult),
                  [PS(ob), "rec"], [("oall", h)])
    OALL = [("oall", h) for h in range(NH)]
    rot["set"] = list(range(8))
    if stage == 3:
        d1 = A.alloc("d1", [128, 2048], F32)
        P.add("dve", lambda e: e.tensor_copy(out=d1[:, 0:1024], in_=oall[:, 0, :]), OALL[:2], ["d1"])
        P.add("dve", lambda e: e.tensor_copy(out=d1[:, 1024:2048], in_=oall[:, 1, :]), OALL[:2], ["d1"])
        dbg_out(d1[:, :], 128, 2048, ["d1"], 0)
        d2 = A.alloc("d2", [128, 2048], F32)
        P.add("dve", lambda e: e.tensor_copy(out=d2[:, 0:1024], in_=Qn[:, :]), [("Qn", 0), ("Qn", 1)], ["d2"])
        P.add("dve", lambda e: e.tensor_copy(out=d2[0:64, 1024:2048], in_=Qr[0:64, :]), [("Qr", 0), ("Qr", 1)], ["d2"])
        dbg_out(d2[:, 0:1024], 128, 1024, ["d2"], 2048)
        dbg_out(d2[0:64, 1024:2048], 64, 1024, ["d2"], 3072)
        P.build()
        return nc, P, A
    for nme in ("Kn", "Krb", "V4", "wv4", "Qn", "Qr", "wkn0", "wkn1", "wqb0", "wqb1", "PT0", "PT1", "PT2", "PT3", "PT4", "PT5",
                "sqk0", "sqk1", "sqk2", "sqq", "rk0", "tmpc0", "t1", "t2", "rec", "tot", "scl0", "scl1", "ssqr_col",
                "ckvn", "cqn", "ropeQ_cos", "ropeQ_sin", "maskb"):
        if nme in A.live:
            A.release(nme)

    if "xgo" not in D1:
        for f_ in d1_front_setup():
            f_()
    xgo, xgh, rso, rsh, tmpa2, xtd, sqd = (D1[k] for k in ("xgo", "xgh", "rso", "rsh", "tmpa2", "xtd", "sqd"))
    XGO = [("xgo", 0), ("xgo", 1)]
    RSO = [("rso", 0), ("rso", 1)]

    pooled = A.alloc("pooled", [128, 8, NTOK], BF16)
    pooled2 = A.alloc("pooled2", [128, 8, NTOK], BF16)
    invc = A.alloc("invc", [128, 4, NTOK], F32)
    ldma(invc[:, :, :], invcnt_d.rearrange("p (g n) -> p g n", g=4), "invc")
    wu = [A.alloc(f"wu{i}", [128, 16, 128], BF16) for i in range(3)]
    ue2 = [A.alloc(f"ue{i}", [128, 8, 144], F32) for i in range(2)]
    sA = A.alloc("sA", [128, 8, 144], F32)
    sB = A.alloc("sB", [128, 8, 144], F32)

    def load_wu(mx):
        wdma(wu[mx % 3][:, :, :], wview(w_in, 1088 + mx * 128, 1088 + (mx + 1) * 128), f"wu{mx % 3}")
    load_wu(0)
    load_wu(1)
    for m in range(8):
        w_, wk_ = wu[m % 3], f"wu{m % 3}"
        ue, uek = ue2[m % 2], f"ue{m % 2}"
        if m + 2 < 8:
            load_wu(m + 2)
        for hh in range(2):
            b = nb()
            for kc in range(16):
                mm(b, banks[b][:, :], w_[:, kc, :], xgo[:, kc, hh * 512:(hh + 1) * 512], kc == 0, kc == 15, [wk_, ("xgo", hh)], kind=2)
            P.add("dve", lambda e, b=b, hh=hh, ue=ue: e.tensor_tensor(out=ue[:, hh * 4:(hh + 1) * 4, 16:144],
                                                               in0=banks[b][:, :].rearrange("p (a n) -> p a n", a=4),
                                                               in1=rso[:, hh * 512:(hh + 1) * 512].rearrange("p (a n) -> p a n", a=4),
                                                               op=ALU.mult),
                  [PS(b), ("rso", hh)], [uek])
        b = nb()
        for kc in range(16):
            mm(b, banks[b][:, 0:128], w_[:, kc, :], xgh[:, kc, :], kc == 0, kc == 15, [wk_, "xgh"], kind=2)
        P.add("dve", lambda e, b=b, ue=ue: e.tensor_tensor(out=ue[:, :, 0:16], in0=banks[b][:, 0:128].rearrange("p (a n) -> p a n", a=8),
                                                    in1=rsh[:, :].rearrange("p (a n) -> p a n", a=8), op=ALU.mult),
              [PS(b), "rsh"], [uek])
        g = m // 2
        nsteps = g + 1
        src, sk = ue, uek
        bufs = [(sA, "sA"), (sB, "sB")]
        for st in range(nsteps):
            sh = 1 << st
            lo = (1 << (st + 1)) - 1
            dst, dk = bufs[st % 2]
            P.add("dve", lambda e, src=src, dst=dst, sh=sh, lo=lo: e.tensor_tensor(out=dst[:, :, lo:144], in0=src[:, :, lo:144],
                                                                                   in1=src[:, :, lo - sh:144 - sh], op=ALU.add),
                  [sk], [dk])
            src, sk = dst, dk
        other, ok = bufs[nsteps % 2]
        P.add("dve", lambda e, src=src, other=other, g=g: e.tensor_tensor(out=other[:, :, 16:144], in0=src[:, :, 16:144],
                                                                          in1=invc[:, g, :].rearrange("p (a n) -> p a n", a=8), op=ALU.mult),
              [sk, "invc"], [ok])
        P.add("dve", lambda e, other=other, m=m, ue=ue: e.tensor_tensor(out=pooled[:, m, :].rearrange("p (a n) -> p a n", a=8),
                                                                 in0=other[:, :, 16:144], in1=ue[:, :, 16:144], op=ALU.subtract),
              [ok, uek], [("pooled", m)])
    wg = A.alloc("wg", [128, 8, 256], BF16)
    wdma(wg[:, :, :], w_pool_grp.rearrange("(kc p) n -> p kc n", p=128), "wg")
    for g in range(4):
        for m2 in range(2):
            for hh in range(2):
                b = nb()
                for k2 in range(2):
                    mm(b, banks[b][:, :], wg[:, g * 2 + k2, m2 * 128:(m2 + 1) * 128], pooled[:, g * 2 + k2, hh * 512:(hh + 1) * 512],
                       k2 == 0, k2 == 1, ["wg", ("pooled", g * 2 + k2)])
                P.add("act", lambda e, b=b, g=g, m2=m2, hh=hh: e.activation(out=pooled2[:, g * 2 + m2, hh * 512:(hh + 1) * 512],
                                                                           in_=banks[b][:, :], func=AF.Copy, scale=C("pool_scale", g * 2 + m2)),
                      [PS(b), "cols"], [("pooled2", g * 2 + m2)])
    PL2 = [("pooled2", i) for i in range(8)]
    if stage == 4:
        d1 = A.alloc("d1", [128, 2048], F32)
        P.add("dve", lambda e: e.tensor_copy(out=d1[:, 0:1024], in_=pooled2[:, 0, :]), PL2, ["d1"])
        P.add("dve", lambda e: e.tensor_copy(out=d1[:, 1024:2048], in_=pooled2[:, 7, :]), PL2, ["d1"])
        dbg_out(d1[:, :], 128, 2048, ["d1"], 0)
        P.build()
        return nc, P, A
    for nme in ("pooled", "invc", "wu0", "wu1", "wu2", "ue0", "ue1", "sA", "sB", "wg", "xgh", "rsh"):
        A.release(nme)

    mixedl = [A.alloc(f"mixed{i}", [128, 8, NTOK], BF16) for i in range(2)]
    wgp = [A.alloc(f"wgp{i}", [128, 16, 128], BF16) for i in range(2)]
    wga = [A.alloc(f"wga{i}", [128, 16, 128], BF16) for i in range(2)]
    wpo = [A.alloc(f"wpo{i}", [128, 8, 128], BF16) for i in range(2)]
    wao = [A.alloc(f"wao{i}", [128, 16, 128], BF16) for i in range(2)]
    lg = [A.alloc(f"lg{i}", [128, 512], F32) for i in range(2)]
    gt = [A.alloc(f"gt{i}", [128, 512], F32) for i in range(2)]
    m1 = [A.alloc(f"m1{i}", [128, 512], F32) for i in range(2)]
    for n in range(16):
        i2 = n % 2
        def load_d2(nx):
            ix = nx % 2
            wdma(wgp[ix][:, :, :], wview(w_in, 2112 + nx * 128, 2112 + (nx + 1) * 128), f"wgp{ix}")
            wdma(wga[ix][:, :, :], wview(w_in, 2112 + D + nx * 128, 2112 + D + (nx + 1) * 128), f"wga{ix}")
            wdma(wpo[ix][:, :, :], wview(w_pool_o, nx * 128, (nx + 1) * 128), f"wpo{ix}")
            wdma(wao[ix][:, :, :], wview(w_attn_o, nx * 128, (nx + 1) * 128), f"wao{ix}")
        if n == 0:
            load_d2(0)
        if n + 1 < 16:
            load_d2(n + 1)
        for hh in range(2):
            T = slice(hh * 512, (hh + 1) * 512)
            bgp, bga, byp, bya = nb(), nb(), nb(), nb()
            for kc in range(16):
                mm(bgp, banks[bgp][:, :], wgp[i2][:, kc, :], xgo[:, kc, T], kc == 0, kc == 15, [f"wgp{i2}", ("xgo", hh)])
            for kc in range(16):
                mm(bga, banks[bga][:, :], wga[i2][:, kc, :], xgo[:, kc, T], kc == 0, kc == 15, [f"wga{i2}", ("xgo", hh)])
            for kc in range(8):
                mm(byp, banks[byp][:, :], wpo[i2][:, kc, :], pooled2[:, kc, T], kc == 0, kc == 7, [f"wpo{i2}", ("pooled2", kc)])
            for kc in range(16):
                mm(bya, banks[bya][:, :], wao[i2][:, kc, :], oall[:, kc, T], kc == 0, kc == 15, [f"wao{i2}", ("oall", kc)])
            for which, bg, by in ((0, bgp, byp), (1, bga, bya)):
                P.add("dve", lambda e, bg=bg, which=which, T=T: e.tensor_tensor(out=lg[which][:, :], in0=banks[bg][:, :], in1=rso[:, T], op=ALU.mult),
                      [PS(bg), ("rso", hh)], [f"lg{which}"])
                P.add("act", lambda e, which=which, n=n: e.activation(out=gt[which][:, :], in_=lg[which][:, :], func=AF.Sigmoid,
                                                                      bias=C("b_gate", which * 16 + n), scale=1.0),
                      [f"lg{which}", "cols"], [f"gt{which}"])
                P.add("dve", lambda e, by=by, which=which: e.tensor_tensor(out=m1[which][:, :], in0=banks[by][:, :], in1=gt[which][:, :], op=ALU.mult),
                      [PS(by), f"gt{which}"], [f"m1{which}"])
            P.add("dve", lambda e, n=n, T=T: e.tensor_tensor(out=mixedl[n // 8][:, n % 8, T], in0=m1[0][:, :], in1=m1[1][:, :], op=ALU.add),
                  ["m10", "m11"], [(f"mixed{n // 8}", n)])
    MIX = [(f"mixed{n // 8}", n) for n in range(16)]
    if stage == 5:
        d1 = A.alloc("d1", [128, 2048], F32)
        P.add("dve", lambda e: e.tensor_copy(out=d1[:, 0:1024], in_=mixedl[0][:, 0, :]), MIX, ["d1"])
        P.add("dve", lambda e: e.tensor_copy(out=d1[:, 1024:2048], in_=mixedl[1][:, 7, :]), MIX, ["d1"])
        dbg_out(d1[:, :], 128, 2048, ["d1"], 0)
        P.build()
        return nc, P, A
    for nme in ("oall", "pooled2", "xgo", "rso", "tmpa2", "lg0", "lg1", "gt0", "gt1", "m10", "m11") + tuple(
            f"{w}{i}" for w in ("wgp", "wga", "wpo", "wao") for i in range(2)):
        A.release(nme)

    h2 = A.alloc("h2", [128, 16, NTOK], BF16)
    rsf = A.alloc("rsf", [128, NTOK], F32)
    wo = [A.alloc(f"wo{i}", [128, 16, 128], BF16) for i in range(3)]
    xm = [A.alloc(f"xm{i}", [128, 512], F32) for i in range(4)]
    rot["set"] = [0, 1, 2, 3, 4, 5]
    SSQ = [6, 7]
    xi = 0
    pend = []
    for n in range(16):
        i2 = n % 3
        def load_wo(nx):
            wdma(wo[nx % 3][:, :, :], wview(w_out, nx * 128, (nx + 1) * 128), f"wo{nx % 3}")
        if n == 0:
            load_wo(0)
            load_wo(1)
        if n + 2 < 16:
            load_wo(n + 2)
        for hh in range(2):
            T = slice(hh * 512, (hh + 1) * 512)
            s = xi % 4
            it = n * 2 + hh
            if it == 0:
                for it2 in (0, 1):
                    ldma(xtd[it2 % 4][:, :], xT_own[(it2 // 2) * 128:(it2 // 2 + 1) * 128, (it2 % 2) * 512:(it2 % 2 + 1) * 512], f"xt{it2 % 4}")
            if it + 2 < 32:
                it2 = it + 2
                ldma(xtd[it2 % 4][:, :], xT_own[(it2 // 2) * 128:(it2 // 2 + 1) * 128, (it2 % 2) * 512:(it2 % 2 + 1) * 512], f"xt{it2 % 4}")
            xi += 1
            b = nb()
            for kc in range(16):
                mm(b, banks[b][:, :], wo[i2][:, kc, :], mixedl[kc // 8][:, kc % 8, T], kc == 0, kc == 15, [f"wo{i2}", (f"mixed{kc // 8}", kc)], kind=2)
            P.add("dve", lambda e, b=b, s=s: e.tensor_tensor(out=xm[s][:, :], in0=banks[b][:, :], in1=xtd[s][:, :], op=ALU.add),
                  [PS(b), f"xt{s}"], [f"xm{s}"])
            P.add("sp", lambda e, s=s, n=n, T=T: e.dma_start(out=xmid_d[n * 128:(n + 1) * 128, T], in_=xm[s][:, :]),
                  [f"xm{s}"], [("xmid", n, hh)], dma=True, semkey=("xmid_st", s))
            P.add("act", lambda e, s=s: e.activation(out=sqd[s][:, :], in_=xm[s][:, :], func=AF.Square),
                  [f"xm{s}"], [f"sq{s}"])
            pend.append((SSQ[hh], s, n))
            if len(pend) > 1:
                pb, ps_, pn = pend.pop(0)
                mm(pb, banks[pb][:, :], ones_bf[:, :], sqd[ps_][:, :], pn == 0, pn == 15, ["ones_bf", f"sq{ps_}"])
            P.add("act", lambda e, s=s, n=n, T=T: e.activation(out=h2[:, n, T], in_=xm[s][:, :], func=AF.Copy, scale=C("g_ffn", n)),
                  [f"xm{s}", "cols"], [("h2", hh)])
    while pend:
        pb, ps_, pn = pend.pop(0)
        mm(pb, banks[pb][:, :], ones_bf[:, :], sqd[ps_][:, :], pn == 0, pn == 15, ["ones_bf", f"sq{ps_}"])
    tmpf = A.alloc("tmpf", [128, 512], F32)
    for hh in range(2):
        rstd_from_psum(SSQ[hh], 512, 1.0 / D, "eps", rsf[:, hh * 512:(hh + 1) * 512], ("rsf", hh), tmpf, "tmpf")
    rot["set"] = list(range(8))
    if stage == 6:
        d1 = A.alloc("d1", [128, 2048], F32)
        P.add("dve", lambda e: e.tensor_copy(out=d1[:, 0:1024], in_=h2[:, 0, :]), [("h2", 0), ("h2", 1)], ["d1"])
        P.add("dve", lambda e: e.tensor_copy(out=d1[:, 1024:2048], in_=rsf[:, :]), [("rsf", 0), ("rsf", 1)], ["d1"])
        dbg_out(d1[:, :], 128, 2048, ["d1"], 0)
        P.build()
        return nc, P, A
    for nme in ("mixed0", "mixed1", "wo0", "wo1", "wo2", "tmpf"):
        A.release(nme)

    NF = DFF // 128
    actl = [A.alloc(f"actT{i}", [128, 11, NTOK], BF16) for i in range(4)]
    wfg = [A.alloc(f"wfg{i}", [128, 16, 128], BF16) for i in range(3)]
    wfu = [A.alloc(f"wfu{i}", [128, 16, 128], BF16) for i in range(3)]
    gts = [A.alloc(f"gts{i}", [128, 512], F32) for i in range(2)]
    uts = [A.alloc(f"uts{i}", [128, 512], F32) for i in range(2)]
    k = 0
    for f in range(NF):
        i2 = f % 3
        def load_f(fx):
            wdma(wfg[fx % 3][:, :, :], wview(w_gate, fx * 128, (fx + 1) * 128), f"wfg{fx % 3}")
            wdma(wfu[fx % 3][:, :, :], wview(w_up, fx * 128, (fx + 1) * 128), f"wfu{fx % 3}")
        if f == 0:
            load_f(0)
            load_f(1)
        if f + 2 < NF:
            load_f(f + 2)
        for hh in range(2):
            T = slice(hh * 512, (hh + 1) * 512)
            j2 = k % 2
            k += 1
            bg, bu = nb(), nb()
            for kc in range(16):
                mm(bg, banks[bg][:, :], wfg[i2][:, kc, :], h2[:, kc, T], kc == 0, kc == 15, [f"wfg{i2}", ("h2", hh)])
            for kc in range(16):
                mm(bu, banks[bu][:, :], wfu[i2][:, kc, :], h2[:, kc, T], kc == 0, kc == 15, [f"wfu{i2}", ("h2", hh)])
            P.add("dve", lambda e, bg=bg, j2=j2, T=T: e.tensor_tensor(out=gts[j2][:, :], in0=banks[bg][:, :], in1=rsf[:, T], op=ALU.mult),
                  [PS(bg), ("rsf", hh)], [f"gts{j2}"])
            P.add("act", lambda e, j2=j2: e.activation(out=gts[j2][:, :], in_=gts[j2][:, :], func=AF.Silu), [f"gts{j2}"], [f"gts{j2}"])
            P.add("dve", lambda e, bu=bu, j2=j2, T=T: e.tensor_tensor(out=uts[j2][:, :], in0=banks[bu][:, :], in1=rsf[:, T], op=ALU.mult),
                  [PS(bu), ("rsf", hh)], [f"uts{j2}"])
            P.add("dve", lambda e, j2=j2, f=f, T=T: e.tensor_tensor(out=actl[f // 11][:, f % 11, T], in0=gts[j2][:, :], in1=uts[j2][:, :], op=ALU.mult),
                  [f"gts{j2}", f"uts{j2}"], [(f"actT{f // 11}", f, hh)])
    for nme in ("h2", "wfg0", "wfg1", "wfg2", "wfu0", "wfu1", "wfu2", "rsf"):
        A.release(nme)
    wd = [A.alloc(f"wd{i}", [128, NF, 128], BF16) for i in range(2)]
    for n in range(16):
        i2 = n % 2
        def load_wd(nx):
            wdma(wd[nx % 2][:, :, :], wview(w_down, nx * 128, (nx + 1) * 128), f"wd{nx % 2}")
        if n == 0:
            load_wd(0)
        if n + 1 < 16:
            load_wd(n + 1)
        for hh in range(2):
            T = slice(hh * 512, (hh + 1) * 512)
            s = xi % 4
            it = n * 2 + hh

            def load_xm(itx):
                sx = (xi0 + itx) % 4
                nx, hx = itx // 2, itx % 2
                P.add("sp", lambda e: e.dma_start(out=xtd[sx][:, :], in_=xmid_d[nx * 128:(nx + 1) * 128, hx * 512:(hx + 1) * 512]),
                      [("xmid", nx, hx)], [f"xt{sx}"], dma=True)
            if it == 0:
                xi0 = xi
                load_xm(0)
                load_xm(1)
            if it + 2 < 32:
                load_xm(it + 2)
            xi += 1
            b = nb()
            for kc in range(NF):
                mm(b, banks[b][:, :], wd[i2][:, kc, :], actl[kc // 11][:, kc % 11, T], kc == 0, kc == NF - 1, [f"wd{i2}", (f"actT{kc // 11}", kc, hh)], kind=2)
            P.add("dve", lambda e, b=b, s=s: e.tensor_tensor(out=xm[s][:, :], in0=banks[b][:, :], in1=xtd[s][:, :], op=ALU.add),
                  [PS(b), f"xt{s}"], [f"xm{s}"])
            P.add("sp", lambda e, s=s, n=n, T=T: e.dma_start(out=outT[n * 128:(n + 1) * 128, T], in_=xm[s][:, :]),
                  [f"xm{s}"], [("OUT", n, hh)], dma=True, semkey=("out_st", s))
    P.build()
    return nc, P, A


def own_token_index(j):
    return np.concatenate([np.arange(128 * (4 * t + j), 128 * (4 * t + j) + 128) for t in range(8)])


def halo_token_index(j):
    return np.concatenate([np.arange(128 * (4 * t + j) - 16, 128 * (4 * t + j)) for t in range(8)])


def make_in_maps(inp):
    f32 = lambda a: np.ascontiguousarray(np.asarray(a, dtype=np.float32))
    x = np.asarray(inp["x"], dtype=np.float32)
    positions = np.asarray(inp["positions"]).astype(np.int32)
    w_in = f32(inp["w_in"][0])
    kr0 = 1024
    w_kr = np.ascontiguousarray(np.concatenate([w_in[:, kr0:kr0 + 64], w_in[:, kr0 + 32:kr0 + 64], w_in[:, kr0:kr0 + 32]], axis=1))
    wqb = f32(inp["w_q_b"][0]).reshape(512, NH, 192)
    w_qb2 = np.ascontiguousarray(np.concatenate([wqb, wqb[:, :, 160:192], wqb[:, :, 128:160]], axis=2).reshape(512, NH * 256))
    shared = {
        "w_in": w_in, "w_kr": w_kr, "w_qb2": w_qb2,
        "w_kn": np.ascontiguousarray(f32(inp["w_kv_b"][0]).reshape(512, NH, 256)[:, :, :128].reshape(512, NH * 128)),
        "w_v": np.ascontiguousarray(f32(inp["w_kv_b"][0]).reshape(512, NH, 256)[:, :, 128:].reshape(512, NH * 128)),
        "w_attn_o": f32(inp["w_attn_o"][0]),
        "w_pool_grp": f32(inp["w_pool_grp"][0]).reshape(1024, 256),
        "w_pool_o": f32(inp["w_pool_o"][0]),
        "w_out": f32(inp["w_out"][0]),
        "w_ffn_gate": f32(inp["w_ffn_gate"][0]),
        "w_ffn_up": f32(inp["w_ffn_up"][0]),
        "w_ffn_down": f32(inp["w_ffn_down"][0]),
    }
    cols = np.zeros((128, CL["_n"]), np.float32)

    def putc(name, arr):
        a = np.asarray(arr, np.float32).reshape(-1, 128).T
        cols[:, CL[name]:CL[name] + a.shape[1]] = a

    putc("g_attn", inp["attn_norm_g"][0])
    putc("g_ffn", inp["ffn_norm_g"][0])
    putc("g_qa", inp["q_a_norm_g"][0])
    putc("g_kva", inp["kv_a_norm_g"][0])
    putc("b_gate", inp["b_gate"][0])
    putc("pool_scale", np.asarray(inp["pool_scale"][0]).reshape(-1))
    gq = np.asarray(inp["q_norm_g"][0], np.float32)
    gk = np.asarray(inp["k_norm_g"][0], np.float32)
    for nm, g in (("gq", gq), ("gk", gk)):
        cols[:, CL[nm + "_n"]] = g[:128]
        cols[:64, CL[nm + "_r"]] = g[128:192]
        cols[:64, CL[nm + "_rs"]] = np.concatenate([g[160:192], g[128:160]])
    half = 32
    inv_freq = (10000.0 ** (-np.arange(half, dtype=np.float32) / half)).astype(np.float32)
    cols[:64, CL["invfreq"]] = np.concatenate([inv_freq, inv_freq])
    cols[:64, CL["nsign"]] = np.concatenate([-np.ones(32), np.ones(32)])
    cols[:, CL["eps"]] = EPS
    cols[:, CL["eps192"]] = 192 * EPS
    cols[:, CL["negpi"]] = -math.pi
    in_maps = []
    wins = (2, 4, 8, 16)
    for c in range(8):
        b, j = c // 4, c % 4
        own = own_token_index(j)
        halo = halo_token_index(j)
        xb = x[b]
        xT_all = np.ascontiguousarray(xb.T)
        xT_own = np.ascontiguousarray(xb[own].T)
        xh = np.where((halo >= 0)[:, None], xb[np.maximum(halo, 0)], 0.0).astype(np.float32)
        xT_halo = np.ascontiguousarray(xh.T)
        pos_all = np.ascontiguousarray(np.broadcast_to(positions[b][None, :], (64, SEQ)))
        pos_own = np.ascontiguousarray(np.broadcast_to(positions[b][own][None, :], (64, NTOK)))
        kk = np.arange(128)[:, None]
        qq = np.arange(128)[None, :]
        masks = np.concatenate([((128 * (cp - j) + kk) <= qq).astype(np.float32) for cp in range(4)], axis=1)
        invcnt = np.stack([1.0 / np.minimum(w, own + 1).astype(np.float32) for w in wins], axis=0).reshape(1, 4 * NTOK)
        invcnt = np.ascontiguousarray(np.broadcast_to(invcnt, (128, 4 * NTOK))).astype(np.float32)
        m = dict(shared)
        m.update({"xT_all": xT_all, "xT_own": xT_own, "xT_halo": xT_halo, "pos_all": pos_all, "pos_own": pos_own,
                  "cols": cols, "masks": np.ascontiguousarray(masks), "invcnt": invcnt})
        in_maps.append(m)
    return in_maps


_CACHE = {}


def kernel(**inputs):
    in_maps = make_in_maps(inputs)
    if "nc" not in _CACHE:
        _CACHE["nc"] = build_program()[0]
    nc = _CACHE["nc"]
    res = run_bass_kernel_spmd(nc, in_maps, core_ids=list(range(8)))
    out = np.zeros((2, SEQ, D), np.float32)
    for c in range(8):
        b, j = c // 4, c % 4
        out[b, own_token_index(j), :] = res.results[c]["outT"].T
    return out
```

```python
import math
import numpy as np
import concourse.bass as bass
import concourse.mybir as mybir
from concourse.bass_utils import run_bass_kernel_spmd

F32 = mybir.dt.float32
BF16 = mybir.dt.bfloat16
I32 = mybir.dt.int32
AF = mybir.ActivationFunctionType
ALU = mybir.AluOpType

D = 2048
SEQ = 4096
NH = 16
DFF = 5632
NTOK = 1024
EPS = 1e-6
SEM_LIM = 30000


class Op:
    __slots__ = ("eng", "fn", "reads", "writes", "is_dma", "deps", "needs_inc", "sem", "val", "idx", "fence", "semkey")

    def __init__(self, eng, fn, reads, writes, is_dma):
        self.eng = eng
        self.fn = fn
        self.reads = reads
        self.writes = writes
        self.is_dma = is_dma
        self.deps = []
        self.needs_inc = False
        self.sem = None
        self.val = 0


class Prog:
    def __init__(self, nc):
        self.nc = nc
        self.ops = []

    def add(self, eng, fn, reads=(), writes=(), dma=False, fence=None, semkey=None):
        op = Op(eng, fn, tuple(reads), tuple(writes), dma)
        op.fence = fence
        op.semkey = semkey
        op.idx = len(self.ops)
        self.ops.append(op)
        return op

    def build(self, final_wait_engine="sp"):
        nc = self.nc
        ops = self.ops
        last_w = {}
        readers = {}
        fence_of = {}
        kbase = lambda k: k if isinstance(k, str) else k[0]
        for op in ops:
            deps = {}
            if op.fence is not None:
                old_bases, new_base = op.fence
                ob = set(old_bases) | {new_base}
                op.writes = tuple(set(op.writes) | set(k for k in set(last_w) | set(readers) if kbase(k) in ob))
                fence_of[new_base] = op
            for k in op.reads + op.writes:
                if k not in last_w and kbase(k) in fence_of and fence_of[kbase(k)] is not op:
                    last_w[k] = fence_of[kbase(k)]
                    readers.setdefault(k, [])
            rset = set(op.reads)
            wset = set(op.writes)
            for k in rset:
                ispsum = isinstance(k, tuple) and k[0] == "ps"
                lw = last_w.get(k)
                if lw is not None:
                    deps[lw.idx] = max(deps.get(lw.idx, 0), 2)
                if ispsum:
                    for r in readers.get(k, ()):
                        if r.eng != op.eng or r.is_dma:
                            deps[r.idx] = max(deps.get(r.idx, 0), 1)
            for k in wset:
                lw = last_w.get(k)
                if lw is not None:
                    deps[lw.idx] = max(deps.get(lw.idx, 0), 1)
                for r in readers.get(k, ()):
                    deps[r.idx] = max(deps.get(r.idx, 0), 1)
            for di, kind in deps.items():
                d = ops[di]
                if d is op:
                    continue
                if d.eng == op.eng and not d.is_dma and not op.is_dma:
                    if op.eng == "pe":
                        continue
                op.deps.append(d)
                d.needs_inc = True
            for k in rset:
                if k not in wset:
                    readers.setdefault(k, []).append(op)
            for k in wset:
                last_w[k] = op
                readers[k] = []
        sem_pool = {}
        counters = {}
        dma_counts = {}

        def get_sem(name):
            if name not in sem_pool:
                sem_pool[name] = nc.alloc_semaphore(name=name)
            return sem_pool[name]

        for op in ops:
            if op.is_dma:
                key = op.semkey if op.semkey is not None else op.writes[0]
                name = "d_" + "_".join(str(x) for x in (key if isinstance(key, tuple) else (key,)))
                op.sem = get_sem(name)
                dma_counts[name] = dma_counts.get(name, 0) + 16
                op.val = dma_counts[name]
                op.needs_inc = True
            elif op.needs_inc:
                c = counters.get(op.eng, 0)
                op.sem = get_sem(f"e_{op.eng}_{c // SEM_LIM}")
                op.val = c % SEM_LIM + 1
                counters[op.eng] = c + 1
        self.n_sems = len(sem_pool)
        streams = {}
        for op in ops:
            streams.setdefault(op.eng, []).append(op)
        final = [o for o in ops if o.is_dma and any(isinstance(k, tuple) and k[0] == "OUT" for k in o.writes)]
        self.stats = {k: len(v) for k, v in streams.items()}

        def emit(engname, e):
            waited = {}
            for op in streams.get(engname, []):
                for d in op.deps:
                    sid = id(d.sem)
                    if waited.get(sid, 0) >= d.val:
                        continue
                    e.wait_ge(d.sem, d.val)
                    waited[sid] = d.val
                ins = op.fn(e)
                if op.needs_inc:
                    ins.then_inc(op.sem, 16 if op.is_dma else 1)
            if engname == final_wait_engine:
                last = {}
                for o in final:
                    prev = last.get(id(o.sem), (None, 0))[1]
                    last[id(o.sem)] = (o.sem, max(o.val, prev))
                for sem, val in last.values():
                    e.wait_ge(sem, val)

        with nc.Block() as block:
            @block.sync
            def _(e):
                emit("sp", e)

            @block.tensor
            def _(e):
                emit("pe", e)

            @block.scalar
            def _(e):
                emit("act", e)

            @block.vector
            def _(e):
                emit("dve", e)

            @block.gpsimd
            def _(e):
                emit("pool", e)


class Arena:
    def __init__(self, P, nc, base, size):
        self.P = P
        self.nc = nc
        self.free = [(base, size)]
        self.live = {}
        self.dead = []
        self.uid = 0
        self.peak = 0
        self.base = base
        self.dummy = nc.alloc_sbuf_tensor_at("arena_dummy", [128, 8], F32, offset=base - 64)

    def alloc(self, name, shape, dtype, keys=None):
        esz = 2 if dtype == BF16 else 4
        n = 1
        for s in shape[1:]:
            n *= s
        size = (n * esz + 63) // 64 * 64
        off = None
        for i, (o, s) in enumerate(self.free):
            if s >= size:
                off = o
                if s == size:
                    self.free.pop(i)
                else:
                    self.free[i] = (o + size, s - size)
                break
        if off is None:
            raise RuntimeError(f"arena OOM allocating {name} {size} free={self.free}")
        self.peak = max(self.peak, off + size - self.base)
        keys = list(keys) if keys is not None else [name]
        old_keys = []
        newdead = []
        for (do, ds, dk) in self.dead:
            lo = max(do, off)
            hi = min(do + ds, off + size)
            if lo < hi:
                old_keys.extend(dk)
                if do < lo:
                    newdead.append((do, lo - do, dk))
                if hi < do + ds:
                    newdead.append((hi, do + ds - hi, dk))
            else:
                newdead.append((do, ds, dk))
        self.dead = newdead
        if old_keys:
            dummy = self.dummy
            self.P.add("pool", lambda e: e.memset(dummy[:, 0:1], 0.0), [], ["arena_dummy"], fence=(list(dict.fromkeys(old_keys)), name))
        self.uid += 1
        h = self.nc.alloc_sbuf_tensor_at(f"{name}_{self.uid}", list(shape), dtype, offset=off)
        self.live[name] = (off, size, keys)
        return h

    def release(self, name):
        off, size, keys = self.live.pop(name)
        self.dead.append((off, size, keys))
        fl = self.free + [(off, size)]
        fl.sort()
        merged = []
        for o, s in fl:
            if merged and merged[-1][0] + merged[-1][1] == o:
                merged[-1] = (merged[-1][0], merged[-1][1] + s)
            else:
                merged.append((o, s))
        self.free = merged


def col_layout():
    L = {}
    n = 0

    def put(name, w):
        nonlocal n
        L[name] = n
        n += w

    put("g_attn", 16)
    put("g_ffn", 16)
    put("g_qa", 4)
    put("g_kva", 4)
    put("b_gate", 32)
    put("pool_scale", 8)
    put("gq_n", 1)
    put("gq_r", 1)
    put("gq_rs", 1)
    put("gk_n", 1)
    put("gk_r", 1)
    put("gk_rs", 1)
    put("invfreq", 1)
    put("nsign", 1)
    put("eps", 1)
    put("eps192", 1)
    put("negpi", 1)
    put("zero", 1)
    L["_n"] = n
    return L


CL = col_layout()


def build_program(stage=99):
    nc = bass.Bass("TRN2", target_bir_lowering=False)
    dt = lambda name, shape, dty=F32, kind="ExternalInput": nc.dram_tensor(name, list(shape), dty, kind=kind).ap()
    xT_all = dt("xT_all", [D, SEQ])
    xT_own = dt("xT_own", [D, NTOK])
    xT_halo = dt("xT_halo", [D, 128])
    pos_all = dt("pos_all", [64, SEQ], I32)
    pos_own = dt("pos_own", [128, NTOK], I32)
    cols_d = dt("cols", [128, CL["_n"]])
    masks_d = dt("masks", [128, 4 * 128])
    invcnt_d = dt("invcnt", [128, 4 * NTOK])
    w_in = dt("w_in", [D, 6208])
    w_kr = dt("w_kr", [D, 128])
    w_qb2 = dt("w_qb2", [512, NH * 256])
    w_kn = dt("w_kn", [512, NH * 128])
    w_v = dt("w_v", [512, NH * 128])
    w_attn_o = dt("w_attn_o", [D, D])
    w_pool_grp = dt("w_pool_grp", [4 * 256, 256])
    w_pool_o = dt("w_pool_o", [1024, D])
    w_out = dt("w_out", [D, D])
    w_gate = dt("w_ffn_gate", [D, DFF])
    w_up = dt("w_ffn_up", [D, DFF])
    w_down = dt("w_ffn_down", [DFF, D])
    outT = dt("outT", [D, NTOK], F32, kind="ExternalOutput")
    xmid_d = dt("xmid_scr", [D, NTOK], F32, kind="Internal")
    dbg = None
    if stage < 99:
        dbg = dt("dbg", [128, 8192], F32, kind="ExternalOutput")

    P = Prog(nc)
    A = Arena(P, nc, 16704, 212000)
    banks = [nc.alloc_psum_tensor(f"bank{i}", [128, 512], F32) for i in range(8)]
    PS = lambda b: ("ps", b)

    rot = {"i": 0, "set": [0, 1, 2, 3, 4, 5, 6, 7]}

    def nb():
        s = rot["set"]
        b = s[rot["i"] % len(s)]
        rot["i"] += 1
        return b

    def wdma(dst_ap, src_ap, key, reads=()):
        P.add("pool", lambda e: e.dma_start(out=dst_ap, in_=src_ap), list(reads), [key], dma=True)

    def ldma(dst_ap, src_ap, key, reads=()):
        P.add("sp", lambda e: e.dma_start(out=dst_ap, in_=src_ap), list(reads), [key], dma=True)

    def mm(bank, out_ap, lhsT, rhs, start, stop, reads, kind=0):
        if kind == 0:
            fn = lambda e: e.matmul(out_ap, lhsT, rhs, start=start, stop=stop)
        elif kind == 1:
            fn = lambda e: e.matmul(out_ap, lhsT, rhs, start=start, stop=stop)
        elif kind == 2:
            fn = lambda e: e.matmul(out_ap, lhsT, rhs, start=start, stop=stop)
        elif kind == 3:
            fn = lambda e: e.matmul(out_ap, lhsT, rhs, start=start, stop=stop)
        elif kind == 4:
            fn = lambda e: e.matmul(out_ap, lhsT, rhs, start=start, stop=stop)
        elif kind == 5:
            fn = lambda e: e.matmul(out_ap, lhsT, rhs, start=start, stop=stop)
        else:
            fn = lambda e: e.matmul(out_ap, lhsT, rhs, start=start, stop=stop)
        P.add("pe", fn, list(reads), [PS(bank)])

    def wview(w_dram, c0, c1):
        return w_dram[:, c0:c1].rearrange("(kc p) n -> p kc n", p=128)

    cols = A.alloc("cols", [128, CL["_n"]], F32)
    ldma(cols[:, :], cols_d, "cols")
    C = lambda name, i=0, p0=0, p1=128: cols[p0:p1, CL[name] + i:CL[name] + i + 1]
    ones_bf = A.alloc("ones_bf", [128, 128], BF16)
    P.add("pool", lambda e: e.memset(ones_bf[:, :], 1.0), [], ["ones_bf"])
    neghalf = A.alloc("neghalf", [128, 512], F32)
    P.add("pool", lambda e: e.memset(neghalf[:, :], -0.5), [], ["neghalf"])
    maskb = A.alloc("maskb", [128, 512], BF16)
    wdma(maskb[:, :], masks_d, "maskb")

    def rstd_from_psum(bank, n, scale, eps_col, out_ap, out_key, tmp, tmp_key, np_=128):
        P.add("act", lambda e: e.activation(out=tmp[0:np_, 0:n], in_=banks[bank][0:np_, 0:n], func=AF.Ln,
                                            bias=C(eps_col, 0, 0, np_), scale=scale),
              [PS(bank), "cols"], [tmp_key])
        P.add("act", lambda e: e.activation(out=out_ap, in_=tmp[0:np_, 0:n], func=AF.Exp, scale=-0.5),
              [tmp_key], [out_key])

    RT = {}

    def rope_setup(tagname):
        RT["pos"] = [A.alloc(f"rt_pos{i}", [128, 512], I32) for i in range(2)]
        for nm in ("rt_ang", "rt_kf", "rt_r1", "rt_r2"):
            RT[nm] = A.alloc(nm, [128, 512], F32)
        RT["gs"] = A.alloc("rt_gs", [128, 2], F32)

    def rope_release():
        for nm in ("rt_pos0", "rt_pos1", "rt_ang", "rt_kf", "rt_r1", "rt_r2", "rt_gs"):
            A.release(nm)

    def rope_groups(pos_d, c, gr, grs, cos_out, sin_out, np_=64):
        TWO_PI = 2.0 * math.pi
        pi_, pk = RT["pos"][c % 2], f"rt_pos{c % 2}"
        ang, kf, r1, r2, gs = RT["rt_ang"], RT["rt_kf"], RT["rt_r1"], RT["rt_r2"], RT["gs"]
        c0 = c * 512
        R = slice(0, np_)

        def fold_clamp(r, rk_):
            P.add("dve", lambda e: e.tensor_single_scalar(out=kf[R, :], in_=r[R, :], scalar=math.pi, op=ALU.is_gt), [rk_], ["rt_kf"])
            P.add("dve", lambda e: e.scalar_tensor_tensor(out=r[R, :], in0=kf[R, :], scalar=-TWO_PI, in1=r[R, :], op0=ALU.mult, op1=ALU.add),
                  ["rt_kf", rk_], [rk_])
            P.add("dve", lambda e: e.tensor_scalar(out=r[R, :], in0=r[R, :], scalar1=math.pi, scalar2=-math.pi, op0=ALU.min, op1=ALU.max),
                  [rk_], [rk_])

        def g_sin():
            st, p0, p1, skey = sin_out
            ldma(pi_[R, :], pos_d[R, c0:c0 + 512], pk)
            P.add("dve", lambda e: e.tensor_tensor(out=gs[R, 0:1], in0=C(grs, 0, 0, np_), in1=C("nsign", 0, 0, np_), op=ALU.mult), ["cols"], ["rt_gs"])
            P.add("dve", lambda e: e.tensor_copy(out=ang[R, :], in_=pi_[R, :]), [pk], ["rt_ang"])
            P.add("dve", lambda e: e.tensor_scalar(out=ang[R, :], in0=ang[R, :], scalar1=C("invfreq", 0, 0, np_), scalar2=None, op0=ALU.mult),
                  ["rt_ang", "cols"], ["rt_ang"])
            P.add("act", lambda e: e.activation(out=kf[R, :], in_=ang[R, :], func=AF.Copy, scale=1.0 / TWO_PI), ["rt_ang"], ["rt_kf"])
            P.add("dve", lambda e: e.tensor_copy(out=pi_[R, :], in_=kf[R, :]), ["rt_kf"], [pk])
            P.add("dve", lambda e: e.tensor_copy(out=kf[R, :], in_=pi_[R, :]), [pk], ["rt_kf"])
            P.add("dve", lambda e: e.scalar_tensor_tensor(out=r1[R, :], in0=kf[R, :], scalar=-TWO_PI, in1=ang[R, :], op0=ALU.mult, op1=ALU.add),
                  ["rt_kf", "rt_ang"], ["rt_r1"])
            fold_clamp(r1, "rt_r1")
            P.add("dve", lambda e: e.tensor_scalar(out=r2[R, :], in0=r1[R, :], scalar1=0.5 * math.pi, scalar2=None, op0=ALU.add), ["rt_r1"], ["rt_r2"])
            P.add("act", lambda e: e.activation(out=r1[R, :], in_=r1[R, :], func=AF.Sin), ["rt_r1"], ["rt_r1"])
            P.add("act", lambda e: e.activation(out=st[p0:p1, c0:c0 + 512], in_=r1[p0:p1, :], func=AF.Copy, scale=gs[p0:p1, 0:1]),
                  ["rt_r1", "rt_gs"], [skey])

        def g_cos():
            ct, p0, p1, ckey = cos_out
            fold_clamp(r2, "rt_r2")
            P.add("act", lambda e: e.activation(out=r2[R, :], in_=r2[R, :], func=AF.Sin), ["rt_r2"], ["rt_r2"])
            P.add("act", lambda e: e.activation(out=ct[p0:p1, c0:c0 + 512], in_=r2[p0:p1, :], func=AF.Copy, scale=C(gr, 0, p0, p1)),
                  ["rt_r2", "cols"], [ckey])

        return [g_sin, g_cos]

    def dbg_out(src_ap, np_, n, reads, col0=0):
        P.add("sp", lambda e: e.dma_start(out=dbg[0:np_, col0:col0 + n], in_=src_ap), list(reads), [("OUT", "dbg", col0)], dma=True)

    ckvn = A.alloc("ckvn", [128, 4, SEQ], BF16)
    krr = A.alloc("krr", [64, SEQ], F32)
    sqr = A.alloc("sqr", [64, SEQ], BF16)
    wkv = A.alloc("wkv", [128, 16, 640], BF16)
    wdma(wkv[:, :, 0:576], wview(w_in, 512, 1088), ("wkv", 0))
    wdma(wkv[:, :, 576:640], wview(w_kr, 64, 128), ("wkv", 1))
    WKV = [("wkv", 0), ("wkv", 1)]
    cosK = A.alloc("ropeK_cos", [64, SEQ], F32)
    sinK = A.alloc("ropeK_sin", [64, SEQ], F32)
    rope_setup("K")
    rgroups = []
    for c in range(8):
        rgroups.extend(rope_groups(pos_all, c, "gk_r", "gk_rs", (cosK, 0, 64, ("ropeK_cos", c)), (sinK, 0, 64, ("ropeK_sin", c))))

    NR = 6
    xt = [A.alloc(f"xt{i}", [128, 512], F32) for i in range(NR)]
    sq = [A.alloc(f"sq{i}", [128, 512], BF16) for i in range(NR)]
    xg = [A.alloc(f"xg{i}", [128, 16, 512], BF16) for i in range(2)]
    ckv = A.alloc("ckv", [128, 4, 512], F32)
    sq2 = A.alloc("sqc", [128, 4, 512], BF16)
    rsx = A.alloc("rsx", [128, 512], F32)
    rskv = A.alloc("rskv", [128, 512], F32)
    tmpa = A.alloc("tmpa", [128, 512], F32)
    tmpa2a = A.alloc("tmpa2a", [128, 512], F32)
    krt = [A.alloc(f"krt{i}", [64, 512], F32) for i in range(2)]
    krs = [A.alloc(f"krs{i}", [64, 512], F32) for i in range(2)]
    rc = [0]

    def nf_load(src_d, t0, n, xgbuf, xgkey, gname, fc):
        s = rc[0] % NR
        rc[0] += 1
        ldma(xt[s][:, 0:n], src_d[fc * 128:(fc + 1) * 128, t0:t0 + n], f"xt{s}")
        P.add("pool", lambda e, s=s: e.tensor_tensor(out=sq[s][:, 0:n], in0=xt[s][:, 0:n], in1=xt[s][:, 0:n], op=ALU.mult),
              [f"xt{s}"], [f"sq{s}"])
        P.add("act", lambda e, s=s: e.activation(out=xgbuf[:, fc, 0:n], in_=xt[s][:, 0:n], func=AF.Copy, scale=C(gname, fc)),
              [f"xt{s}", "cols"], [xgkey])
        return s

    def nf_mm(s, n, fc, bssq):
        mm(bssq, banks[bssq][:, 0:n], ones_bf[:, :], sq[s][:, 0:n], fc == 0, fc == 15, ["ones_bf", f"sq{s}"], kind=1)

    def nf_chunk(src_d, t0, n, xgbuf, xgkey, gname, fc, bssq):
        s = nf_load(src_d, t0, n, xgbuf, xgkey, gname, fc)
        nf_mm(s, n, fc, bssq)

    bs_next = nb()
    for fc in range(16):
        nf_chunk(xT_all, 0, 512, xg[0], "xg0", "g_attn", fc, bs_next)
    rgroups.pop(0)()
    rgroups.pop(0)()
    for tt in range(8):
        xb = xg[tt % 2]
        xk = f"xg{tt % 2}"
        bssq = bs_next
        rstd_from_psum(bssq, 512, 1.0 / D, "eps", rsx[:, :], "rsx", tmpa, "tmpa")
        T = slice(tt * 512, (tt + 1) * 512)
        k2 = tt % 2
        if tt + 1 < 8:
            bs_next = nb()
        nxt = list(range(16))
        pend_mm = []
        for m in range(6):
            b = nb()
            if m < 4:
                for kc in range(16):
                    mm(b, banks[b][:, :], wkv[:, kc, m * 128:(m + 1) * 128], xb[:, kc, :], kc == 0, kc == 15, WKV + [xk], kind=2)
                P.add("dve", lambda e, b=b, m=m: e.tensor_tensor(out=ckv[:, m, :], in0=banks[b][:, :], in1=rsx[:, :], op=ALU.mult),
                      [PS(b), "rsx"], [("ckv", m)])
                P.add("dve", lambda e, m=m: e.tensor_tensor(out=sq2[:, m, :], in0=ckv[:, m, :], in1=ckv[:, m, :], op=ALU.mult),
                      [("ckv", m)], [("sqc", m)])
            else:
                c0 = 512 + (m - 4) * 64
                for kc in range(16):
                    mm(b, banks[b][0:64, :], wkv[:, kc, c0:c0 + 64], xb[:, kc, :], kc == 0, kc == 15, WKV + [xk], kind=2)
                dst, dk = (krt[k2], f"krt{k2}") if m == 4 else (krs[k2], f"krs{k2}")
                P.add("dve", lambda e, b=b, dst=dst: e.tensor_tensor(out=dst[:, :], in0=banks[b][0:64, :], in1=rsx[0:64, :], op=ALU.mult),
                      [PS(b), "rsx"], [dk])
                if m == 4:
                    P.add("dve", lambda e, T=T, dst=dst: e.tensor_tensor(out=sqr[:, T], in0=dst[:, :], in1=dst[:, :], op=ALU.mult),
                          [dk], [("sqr", tt)])
            for (s_, fc_) in pend_mm:
                nf_mm(s_, 512, fc_, bs_next)
            pend_mm = []
            if tt + 1 < 8:
                for _ in range(3 if m < 5 else 1):
                    if nxt:
                        fc = nxt.pop(0)
                        s_ = nf_load(xT_all, (tt + 1) * 512, 512, xg[(tt + 1) % 2], f"xg{(tt + 1) % 2}", "g_attn", fc)
                        pend_mm.append((s_, fc))
            if rgroups and m in (1, 3):
                rgroups.pop(0)()
        for (s_, fc_) in pend_mm:
            nf_mm(s_, 512, fc_, bs_next)
        b = nb()
        for m in range(4):
            mm(b, banks[b][:, :], ones_bf[:, :], sq2[:, m, :], m == 0, m == 3, ["ones_bf", ("sqc", m)], kind=1)
        rstd_from_psum(b, 512, 1.0 / 512, "eps", rskv[:, :], "rskv", tmpa2a, "tmpa2a")
        for m in range(4):
            P.add("dve", lambda e, m=m, T=T: e.scalar_tensor_tensor(out=ckvn[:, m, T], in0=ckv[:, m, :], scalar=C("g_kva", m),
                                                                     in1=rskv[:, :], op0=ALU.mult, op1=ALU.mult),
                  [("ckv", m), "cols", "rskv"], [("ckvn", tt)])
        P.add("pool", lambda e, T=T, k2=k2: e.tensor_tensor(out=krt[k2][:, :], in0=krt[k2][:, :], in1=cosK[:, T], op=ALU.mult),
              [f"krt{k2}", ("ropeK_cos", tt)], [f"krt{k2}"])
        P.add("pool", lambda e, T=T, k2=k2: e.tensor_tensor(out=krs[k2][:, :], in0=krs[k2][:, :], in1=sinK[:, T], op=ALU.mult),
              [f"krs{k2}", ("ropeK_sin", tt)], [f"krs{k2}"])
        P.add("pool", lambda e, T=T, k2=k2: e.tensor_tensor(out=krr[:, T], in0=krt[k2][:, :], in1=krs[k2][:, :], op=ALU.add),
              [f"krt{k2}", f"krs{k2}"], [("krr", tt)])
    CKVN = [("ckvn", t) for t in range(8)]
    KRR = [("krr", t) for t in range(8)]
    SQR = [("sqr", t) for t in range(8)]
    if stage == 1:
        d1 = A.alloc("d1", [128, 2048], F32)
        P.add("dve", lambda e: e.tensor_copy(out=d1[:, :], in_=ckvn[:, 0, 0:2048]), CKVN, ["d1"])
        dbg_out(d1[:, :], 128, 2048, ["d1"], 0)
        dbg_out(krr[:, 0:2048], 64, 2048, KRR, 2048)
        dbg_out(cosK[:, 0:2048], 64, 2048, [("ropeK_cos", i) for i in range(4)], 4096)
        dbg_out(sinK[:, 0:2048], 64, 2048, [("ropeK_sin", i) for i in range(4)], 6144)
        P.build()
        return nc, P, A
    for nme in ("wkv", "ckv", "sqc", "rskv", "krt0", "krt1", "krs0", "krs1", "ropeK_cos", "ropeK_sin"):
        A.release(nme)

    cqn = A.alloc("cqn", [128, 4, NTOK], BF16)
    wq = A.alloc("wq", [128, 16, 512], BF16)
    wdma(wq[:, :, :], wview(w_in, 0, 512), "wq")
    csQ = A.alloc("ropeQ_cs", [128, NTOK], F32)
    for c in range(2):
        for g_ in rope_groups(pos_own, c, "gq_r", "gq_rs", (csQ, 0, 64, ("ropeQ_cs", "c", c)), (csQ, 64, 128, ("ropeQ_cs", "s", c)), np_=128):
            g_()
    cq = A.alloc("cq", [128, 4, 512], F32)
    sq2b = A.alloc("sq2b", [128, 4, 512], BF16)
    rsq = A.alloc("rsq", [128, 512], F32)
    bs_next = nb()
    for fc in range(16):
        nf_chunk(xT_own, 0, 512, xg[0], "xg0", "g_attn", fc, bs_next)
    for hh in range(2):
        xb = xg[hh]
        xk = f"xg{hh}"
        bssq = bs_next
        rstd_from_psum(bssq, 512, 1.0 / D, "eps", rsx[:, :], "rsx", tmpa, "tmpa")
        T = slice(hh * 512, (hh + 1) * 512)
        if hh == 0:
            bs_next = nb()
        nxt = list(range(16))
        for m in range(4):
            b = nb()
            for kc in range(16):
                mm(b, banks[b][:, :], wq[:, kc, m * 128:(m + 1) * 128], xb[:, kc, :], kc == 0, kc == 15, ["wq", xk], kind=2)
            P.add("dve", lambda e, b=b, m=m: e.tensor_tensor(out=cq[:, m, :], in0=banks[b][:, :], in1=rsx[:, :], op=ALU.mult),
                  [PS(b), "rsx"], [("cq", m)])
            P.add("act", lambda e, m=m: e.activation(out=sq2b[:, m, :], in_=cq[:, m, :], func=AF.Square),
                  [("cq", m)], [("sq2b", m)])
            if hh == 0:
                for _ in range(4):
                    fc = nxt.pop(0)
                    nf_chunk(xT_own, 512, 512, xg[1], "xg1", "g_attn", fc, bs_next)
        b = nb()
        for m in range(4):
            mm(b, banks[b][:, :], ones_bf[:, :], sq2b[:, m, :], m == 0, m == 3, ["ones_bf", ("sq2b", m)], kind=1)
        rstd_from_psum(b, 512, 1.0 / 512, "eps", rsq[:, :], "rsq", tmpa2a, "tmpa2a")
        for m in range(4):
            P.add("dve", lambda e, m=m, T=T: e.scalar_tensor_tensor(out=cqn[:, m, T], in0=cq[:, m, :], scalar=C("g_qa", m),
                                                                     in1=rsq[:, :], op0=ALU.mult, op1=ALU.mult),
                  [("cq", m), "cols", "rsq"], [("cqn", hh)])
    CQN = [("cqn", 0), ("cqn", 1)]
    if stage == 2:
        d1 = A.alloc("d1", [128, 1024], F32)
        P.add("dve", lambda e: e.tensor_copy(out=d1[:, :], in_=cqn[:, 0, :]), CQN, ["d1"])
        dbg_out(d1[:, :], 128, 1024, ["d1"], 0)
        dbg_out(csQ[0:64, :], 64, 1024, [("ropeQ_cs", "c", 0), ("ropeQ_cs", "c", 1)], 1024)
        P.build()
        return nc, P, A
    rope_release()
    for nme in ("wq", "cq", "sq2b", "rsq", "rsx", "tmpa2a") + tuple(f"xt{i}" for i in range(NR)) + tuple(f"sq{i}" for i in range(NR)) + ("xg0", "xg1"):
        A.release(nme)

    oall = A.alloc("oall", [128, NH, NTOK], BF16)
    Krb = A.alloc("Krb", [128, SEQ], BF16)
    ssqr_col = A.alloc("ssqr_col", [128, 32], F32)
    COLB = 3
    for tt in range(8):
        T = slice(tt * 512, (tt + 1) * 512)
        P.add("act", lambda e, T=T: e.activation(out=Krb[0:64, T], in_=krr[:, T], func=AF.Copy), [("krr", tt)], [("Krb", tt)])
    P.add("sp", lambda e: e.dma_start(out=Krb[64:128, :], in_=Krb[0:64, :]), [("Krb", t_) for t_ in range(8)], [("Krb", "pad")], dma=True)
    for kc in range(32):
        mm(COLB, banks[COLB][:, kc:kc + 1], sqr[:, kc * 128:(kc + 1) * 128], ones_bf[0:64, 0:1], True, True, [("sqr", kc // 4), "ones_bf"])
    P.add("dve", lambda e: e.tensor_copy(out=ssqr_col[:, :], in_=banks[COLB][:, 0:32]), [PS(COLB)], ["ssqr_col"])
    P.add("pool", lambda e: e.memset(A.dummy[:, 1:2], 0.0), [("krr", t) for t in range(8)] + [("sqr", t) for t in range(8)], ["krr_done"])
    A.release("krr")
    A.release("sqr")
    Kn = A.alloc("Kn", [128, SEQ], BF16)
    V4 = A.alloc("V4", [128, 32, 512], BF16)
    Qn = A.alloc("Qn", [128, NTOK], BF16)
    Qr = A.alloc("Qr", [128, NTOK], BF16)
    wkn = [A.alloc(f"wkn{i}", [128, 4, 128], BF16) for i in range(2)]
    wv4 = A.alloc("wv4", [128, 4, 512], BF16)
    wqb = [A.alloc(f"wqb{i}", [128, 4, 256], BF16) for i in range(2)]
    NPT = 6
    PT = [A.alloc(f"PT{i}", [128, 512], BF16) for i in range(NPT)]
    sqk = [A.alloc(f"sqk{i}", [128, 512], BF16) for i in range(3)]
    sqq = A.alloc("sqq", [128, 512], BF16)
    P.add("pool", lambda e: e.memset(sqq[:, :], 0.0), [], ["sqq"])
    rk = [A.alloc(f"rk{i}", [128, 512], F32) for i in range(1)]
    tmpc = [A.alloc(f"tmpc{i}", [128, 512], F32) for i in range(1)]
    t1 = A.alloc("t1", [128, 512], F32)
    rec = A.alloc("rec", [128, 512], F32)
    tot = A.alloc("tot", [128, 32], F32)
    scl = [A.alloc(f"scl{i}", [128, 32], F32) for i in range(2)]
    SC = math.sqrt(192.0)
    pti = 0
    nheads = NH if stage != 3 else 2
    GEN = [0, 1, 2, 4, 5, 6, 7]
    SB = [0, 1, 2]
    OB = [4, 5]
    DB = [6, 7]

    def load_head(hx):
        wdma(wkn[hx % 2][:, :, :], wview(w_kn, hx * 128, (hx + 1) * 128), f"wkn{hx % 2}")
        wdma(wqb[hx % 2][:, :, :], wview(w_qb2, hx * 256, (hx + 1) * 256), f"wqb{hx % 2}")

    D1 = {}

    def d1_front_setup():
        A.release("ckvn")
        A.release("cqn")
        D1["xgo"] = A.alloc("xgo", [128, 16, NTOK], BF16)
        D1["xgh"] = A.alloc("xgh", [128, 16, 128], BF16)
        D1["rso"] = A.alloc("rso", [128, NTOK], F32)
        D1["rsh"] = A.alloc("rsh", [128, 128], F32)
        D1["tmpa2"] = A.alloc("tmpa2", [128, 512], F32)
        D1["xtd"] = [A.alloc(f"xt{i}", [128, 512], F32) for i in range(4)]
        D1["sqd"] = [A.alloc(f"sq{i}", [128, 512], BF16) for i in range(4)]
        xgo_, xgh_, rso_, rsh_, tmpa2_, xtd_, sqd_ = (D1[k] for k in ("xgo", "xgh", "rso", "rsh", "tmpa2", "xtd", "sqd"))
        out = []
        cnt = [0]

        def chunk(src_d, t0, n, dst, dkey, c0, fc, last_fn):
            def f():
                s_ = cnt[0] % 4
                cnt[0] += 1
                ldma(xtd_[s_][:, 0:n], src_d[fc * 128:(fc + 1) * 128, t0:t0 + n], f"xt{s_}")
                P.add("dve", lambda e: e.tensor_tensor(out=sqd_[s_][:, 0:n], in0=xtd_[s_][:, 0:n], in1=xtd_[s_][:, 0:n], op=ALU.mult),
                      [f"xt{s_}"], [f"sq{s_}"])
                P.add("act", lambda e: e.activation(out=dst[:, fc, c0:c0 + n], in_=xtd_[s_][:, 0:n], func=AF.Copy, scale=C("g_attn", fc)),
                      [f"xt{s_}", "cols"], [dkey])
                mm(COLB, banks[COLB][:, 0:n], ones_bf[:, :], sqd_[s_][:, 0:n], fc == 0, fc == 15, ["ones_bf", f"sq{s_}"], kind=1)
                if fc == 15:
                    last_fn()
            return f

        for hh in range(2):
            lf = (lambda hh=hh: rstd_from_psum(COLB, 512, 1.0 / D, "eps", rso_[:, hh * 512:(hh + 1) * 512], ("rso", hh), tmpa2_, "tmpa2"))
            for fc in range(16):
                out.append(chunk(xT_own, hh * 512, 512, xgo_, ("xgo", hh), hh * 512, fc, lf))
        lf = (lambda: rstd_from_psum(COLB, 128, 1.0 / D, "eps", rsh_[:, :], "rsh", tmpa2_, "tmpa2"))
        for fc in range(16):
            out.append(chunk(xT_halo, 0, 128, xgh_, "xgh", 0, fc, lf))
        return out

    for h in range(nheads):
        wk_, wkk = wkn[h % 2], f"wkn{h % 2}"
        wq_, wqk = wqb[h % 2], f"wqb{h % 2}"
        sc_, sck = scl[h % 2], f"scl{h % 2}"
        if h == 0:
            load_head(0)
        if h + 1 < nheads:
            load_head(h + 1)
        rot["set"] = GEN
        rot["i"] = 0
        for hh in range(2):
            T = slice(hh * 512, (hh + 1) * 512)
            bn, brs = nb(), nb()
            for kc in range(4):
                mm(bn, banks[bn][:, :], wq_[:, kc, 0:128], cqn[:, kc, T], kc == 0, kc == 3, [wqk, ("cqn", hh)], kind=3)
            for kc in range(4):
                mm(brs, banks[brs][:, :], wq_[:, kc, 128:256], cqn[:, kc, T], kc == 0, kc == 3, [wqk, ("cqn", hh)], kind=3)
            P.add("act", lambda e, bn=bn: e.activation(out=sqk[0][:, :], in_=banks[bn][:, :], func=AF.Square), [PS(bn)], ["sqk0"])
            P.add("act", lambda e, brs=brs: e.activation(out=sqq[0:64, :], in_=banks[brs][0:64, :], func=AF.Square), [PS(brs)], ["sqq"])
            bq = nb()
            mm(bq, banks[bq][:, :], ones_bf[:, :], sqk[0][:, :], True, False, ["ones_bf", "sqk0"], kind=1)
            mm(bq, banks[bq][:, :], ones_bf[:, :], sqq[:, :], False, True, ["ones_bf", "sqq"], kind=1)
            rstd_from_psum(bq, 512, 1.0, "eps192", rk[0][:, :], "rk0", tmpc[0], "tmpc0")
            P.add("dve", lambda e, bn=bn, T=T: e.scalar_tensor_tensor(out=Qn[:, T], in0=banks[bn][:, :], scalar=C("gq_n"),
                                                                      in1=rk[0][:, :], op0=ALU.mult, op1=ALU.mult),
                  [PS(bn), "cols", "rk0"], [("Qn", hh)])
            P.add("dve", lambda e, brs=brs, T=T: e.tensor_tensor(out=t1[:, :], in0=banks[brs][:, :], in1=csQ[:, T], op=ALU.mult),
                  [PS(brs), ("ropeQ_cs", "c", hh), ("ropeQ_cs", "s", hh)], ["t1"])
            P.add("dve", lambda e, T=T: e.tensor_tensor(out=Qr[:, T], in0=t1[:, :], in1=rk[0][:, :], op=ALU.mult),
                  ["t1", "rk0"], [("Qr", hh)])
        if h % 4 == 0:
            wdma(wv4[:, :, :], wview(w_v, h * 128, (h + 4) * 128), "wv4")
            for tc in range(32):
                bv = nb()
                for kc in range(4):
                    mm(bv, banks[bv][:, :], ckvn[:, kc, tc * 128:(tc + 1) * 128], wv4[:, kc, :], kc == 0, kc == 3, ["wv4", ("ckvn", tc // 4)])
                if tc % 2 == 0:
                    P.add("act", lambda e, bv=bv, tc=tc: e.activation(out=V4[:, tc, :], in_=banks[bv][:, :], func=AF.Copy), [PS(bv)], [("V4", tc // 4)])
                else:
                    P.add("dve", lambda e, bv=bv, tc=tc: e.tensor_copy(out=V4[:, tc, :], in_=banks[bv][:, :]), [PS(bv)], [("V4", tc // 4)])
        hl = h % 4
        def k_mm(tt):
            bk = nb()
            T = slice(tt * 512, (tt + 1) * 512)
            for kc in range(4):
                mm(bk, banks[bk][:, :], wk_[:, kc, :], ckvn[:, kc, T], kc == 0, kc == 3, [wkk, ("ckvn", tt)])
            i3 = tt % 3
            P.add("act", lambda e, bk=bk, i3=i3: e.activation(out=sqk[i3][:, :], in_=banks[bk][:, :], func=AF.Square), [PS(bk)], [f"sqk{i3}"])
            P.add("dve", lambda e, bk=bk, T=T: e.tensor_scalar(out=Kn[:, T], in0=banks[bk][:, :], scalar1=C("gk_n"), scalar2=None, op0=ALU.mult),
                  [PS(bk), "cols"], [("Kn", tt)])

        def k_col(tt):
            i3 = tt % 3
            for c in range(4):
                kc = tt * 4 + c
                mm(COLB, banks[COLB][:, kc:kc + 1], sqk[i3][:, c * 128:(c + 1) * 128], ones_bf[:, 0:1], True, True, [f"sqk{i3}", "ones_bf"])

        k_mm(0)
        for tt in range(8):
            if tt + 1 < 8:
                k_mm(tt + 1)
            k_col(tt)
        P.add("dve", lambda e: e.tensor_tensor(out=tot[:, :], in0=banks[COLB][:, 0:32], in1=ssqr_col[:, :], op=ALU.add), [PS(COLB), "ssqr_col"], ["tot"])
        P.add("act", lambda e: e.activation(out=tot[:, :], in_=tot[:, :], func=AF.Ln, bias=C("eps192"), scale=1.0), ["tot", "cols"], ["tot"])
        P.add("act", lambda e: e.activation(out=tot[:, :], in_=tot[:, :], func=AF.Exp, scale=-0.5), ["tot"], ["tot"])
        P.add("dve", lambda e, sc_=sc_: e.tensor_scalar(out=sc_[:, :], in0=tot[:, :], scalar1=SC, scalar2=None, op0=ALU.mult), ["tot"], [sck])
        extra = []
        if h == NH - 1 and stage > 3:
            extra = d1_front_setup()
        items = []
        for t in range(8):
            pieces = [(128 * t, 512, 0), (512, 1024, 1)] if t < 4 else [(128 * t, 1024, 1)]
            for c in range(4):
                for (q0, q1, half) in pieces:
                    items.append((t, c, q0, q1, half))
        rest = items[2:]
        big = [it_ for it_ in rest if it_[3] - it_[2] >= 384]
        small = [it_ for it_ in rest if it_[3] - it_[2] < 384]
        items = items[:2]
        while big or small:
            for _ in range(2):
                if big:
                    items.append(big.pop(0))
            if small:
                items.append(small.pop(0))
        first_idx = {}
        last_idx = {}
        for ii, it_ in enumerate(items):
            first_idx.setdefault(it_[4], ii)
            last_idx[it_[4]] = ii
        sbank = {}

        def emit_S(i):
            t, c, q0, q1, half = items[i]
            kc = 4 * t + c
            KS = slice(kc * 128, (kc + 1) * 128)
            n = q1 - q0
            bS = SB[i % 3]
            sbank[i] = bS
            mm(bS, banks[bS][:, 0:n], Kn[:, KS], Qn[:, q0:q1], True, False, [("Kn", t), ("Qn", half)], kind=4)
            mm(bS, banks[bS][:, 0:n], Krb[:, KS], Qr[:, q0:q1], False, True, [("Krb", t), ("Krb", "pad"), ("Qr", half)], kind=4)

        LA = 2
        for i in range(min(LA, len(items))):
            emit_S(i)
        for i in range(len(items)):
            if i + LA < len(items):
                emit_S(i + LA)
            t, c, q0, q1, half = items[i]
            kc = 4 * t + c
            n = q1 - q0
            bS = sbank[i]
            pt = PT[pti % NPT]
            pk = f"PT{pti % NPT}"
            pti += 1
            P.add("act", lambda e, bS=bS, pt=pt, n=n, kc=kc, sc_=sc_: e.activation(out=pt[:, 0:n], in_=banks[bS][:, 0:n], func=AF.Exp,
                                                                                 scale=sc_[:, kc:kc + 1]),
                  [PS(bS), sck], [pk])
            if q0 == 128 * t:
                P.add("dve", lambda e, pt=pt, c=c: e.tensor_tensor(out=pt[:, 0:128], in0=pt[:, 0:128],
                                                                  in1=maskb[:, c * 128:(c + 1) * 128], op=ALU.mult),
                      [pk, "maskb"], [pk])
            first = (first_idx[half] == i)
            last = (last_idx[half] == i)
            ob, db = OB[half], DB[half]
            o0 = q0 - 512 * half
            mm(ob, banks[ob][:, o0:o0 + n], V4[:, kc, hl * 128:(hl + 1) * 128], pt[:, 0:n], first, last, [("V4", kc // 4), pk], kind=5)
            mm(db, banks[db][:, o0:o0 + n], ones_bf[:, :], pt[:, 0:n], first, last, ["ones_bf", pk], kind=6)
            if extra:
                extra.pop(0)()
        while extra:
            extra.pop(0)()
        for half in range(2):
            ob, db = OB[half], DB[half]
            P.add("act", lambda e, db=db: e.activation(out=rec[:, :], in_=banks[db][:, :], func=AF.Ln), [PS(db)], ["rec"])
            P.add("act", lambda e: e.activation(out=rec[:, :], in_=rec[:, :], func=AF.Exp, scale=-1.0), ["rec"], ["rec"])
            P.add("dve", lambda e, ob=ob, half=half, h=h: e.tensor_tensor(out=oall[:, h, half * 512:(half + 1) * 512],
                                                                           in0=banks[ob][:, :], in1=rec[:, :], op=ALU.mult),
                  [PS(ob), "rec"], [("oall", h)])
    OALL = [("oall", h) for h in range(NH)]
    rot["set"] = list(range(8))
    if stage == 3:
        d1 = A.alloc("d1", [128, 2048], F32)
        P.add("dve", lambda e: e.tensor_copy(out=d1[:, 0:1024], in_=oall[:, 0, :]), OALL[:2], ["d1"])
        P.add("dve", lambda e: e.tensor_copy(out=d1[:, 1024:2048], in_=oall[:, 1, :]), OALL[:2], ["d1"])
        dbg_out(d1[:, :], 128, 2048, ["d1"], 0)
        d2 = A.alloc("d2", [128, 2048], F32)
        P.add("dve", lambda e: e.tensor_copy(out=d2[:, 0:1024], in_=Qn[:, :]), [("Qn", 0), ("Qn", 1)], ["d2"])
        P.add("dve", lambda e: e.tensor_copy(out=d2[0:64, 1024:2048], in_=Qr[0:64, :]), [("Qr", 0), ("Qr", 1)], ["d2"])
        dbg_out(d2[:, 0:1024], 128, 1024, ["d2"], 2048)
        dbg_out(d2[0:64, 1024:2048], 64, 1024, ["d2"], 3072)
        P.build()
        return nc, P, A
    for nme in ("Kn", "Krb", "V4", "wv4", "Qn", "Qr", "wkn0", "wkn1", "wqb0", "wqb1", "PT0", "PT1", "PT2", "PT3", "PT4", "PT5",
                "sqk0", "sqk1", "sqk2", "sqq", "rk0", "tmpc0", "t1", "rec", "tot", "scl0", "scl1", "ssqr_col",
                "ckvn", "cqn", "ropeQ_cs", "maskb"):
        if nme in A.live:
            A.release(nme)

    if "xgo" not in D1:
        for f_ in d1_front_setup():
            f_()
    xgo, xgh, rso, rsh, tmpa2, xtd, sqd = (D1[k] for k in ("xgo", "xgh", "rso", "rsh", "tmpa2", "xtd", "sqd"))
    XGO = [("xgo", 0), ("xgo", 1)]
    RSO = [("rso", 0), ("rso", 1)]

    pooled = A.alloc("pooled", [128, 8, NTOK], BF16)
    pooled2 = A.alloc("pooled2", [128, 8, NTOK], BF16)
    invc = A.alloc("invc", [128, 4, NTOK], F32)
    ldma(invc[:, :, :], invcnt_d.rearrange("p (g n) -> p g n", g=4), "invc")
    wu = [A.alloc(f"wu{i}", [128, 16, 128], BF16) for i in range(3)]
    ue2 = [A.alloc(f"ue{i}", [128, 8, 144], F32) for i in range(2)]
    sA = A.alloc("sA", [128, 8, 144], F32)
    sB = A.alloc("sB", [128, 8, 144], F32)

    def load_wu(mx):
        wdma(wu[mx % 3][:, :, :], wview(w_in, 1088 + mx * 128, 1088 + (mx + 1) * 128), f"wu{mx % 3}")
    load_wu(0)
    load_wu(1)
    for m in range(8):
        w_, wk_ = wu[m % 3], f"wu{m % 3}"
        ue, uek = ue2[m % 2], f"ue{m % 2}"
        if m + 2 < 8:
            load_wu(m + 2)
        for hh in range(2):
            b = nb()
            for kc in range(16):
                mm(b, banks[b][:, :], w_[:, kc, :], xgo[:, kc, hh * 512:(hh + 1) * 512], kc == 0, kc == 15, [wk_, ("xgo", hh)], kind=2)
            P.add("dve", lambda e, b=b, hh=hh, ue=ue: e.tensor_tensor(out=ue[:, hh * 4:(hh + 1) * 4, 16:144],
                                                               in0=banks[b][:, :].rearrange("p (a n) -> p a n", a=4),
                                                               in1=rso[:, hh * 512:(hh + 1) * 512].rearrange("p (a n) -> p a n", a=4),
                                                               op=ALU.mult),
                  [PS(b), ("rso", hh)], [uek])
        b = nb()
        for kc in range(16):
            mm(b, banks[b][:, 0:128], w_[:, kc, :], xgh[:, kc, :], kc == 0, kc == 15, [wk_, "xgh"], kind=2)
        P.add("dve", lambda e, b=b, ue=ue: e.tensor_tensor(out=ue[:, :, 0:16], in0=banks[b][:, 0:128].rearrange("p (a n) -> p a n", a=8),
                                                    in1=rsh[:, :].rearrange("p (a n) -> p a n", a=8), op=ALU.mult),
              [PS(b), "rsh"], [uek])
        g = m // 2
        nsteps = g + 1
        src, sk = ue, uek
        bufs = [(sA, "sA"), (sB, "sB")]
        for st in range(nsteps):
            sh = 1 << st
            lo = (1 << (st + 1)) - 1
            dst, dk = bufs[st % 2]
            P.add("dve", lambda e, src=src, dst=dst, sh=sh, lo=lo: e.tensor_tensor(out=dst[:, :, lo:144], in0=src[:, :, lo:144],
                                                                                   in1=src[:, :, lo - sh:144 - sh], op=ALU.add),
                  [sk], [dk])
            src, sk = dst, dk
        other, ok = bufs[nsteps % 2]
        P.add("dve", lambda e, src=src, other=other, g=g: e.tensor_tensor(out=other[:, :, 16:144], in0=src[:, :, 16:144],
                                                                          in1=invc[:, g, :].rearrange("p (a n) -> p a n", a=8), op=ALU.mult),
              [sk, "invc"], [ok])
        P.add("dve", lambda e, other=other, m=m, ue=ue: e.tensor_tensor(out=pooled[:, m, :].rearrange("p (a n) -> p a n", a=8),
                                                                 in0=other[:, :, 16:144], in1=ue[:, :, 16:144], op=ALU.subtract),
              [ok, uek], [("pooled", m)])
    wg = A.alloc("wg", [128, 8, 256], BF16)
    wdma(wg[:, :, :], w_pool_grp.rearrange("(kc p) n -> p kc n", p=128), "wg")
    for g in range(4):
        for m2 in range(2):
            for hh in range(2):
                b = nb()
                for k2 in range(2):
                    mm(b, banks[b][:, :], wg[:, g * 2 + k2, m2 * 128:(m2 + 1) * 128], pooled[:, g * 2 + k2, hh * 512:(hh + 1) * 512],
                       k2 == 0, k2 == 1, ["wg", ("pooled", g * 2 + k2)])
                P.add("act", lambda e, b=b, g=g, m2=m2, hh=hh: e.activation(out=pooled2[:, g * 2 + m2, hh * 512:(hh + 1) * 512],
                                                                           in_=banks[b][:, :], func=AF.Copy, scale=C("pool_scale", g * 2 + m2)),
                      [PS(b), "cols"], [("pooled2", g * 2 + m2)])
    PL2 = [("pooled2", i) for i in range(8)]
    if stage == 4:
        d1 = A.alloc("d1", [128, 2048], F32)
        P.add("dve", lambda e: e.tensor_copy(out=d1[:, 0:1024], in_=pooled2[:, 0, :]), PL2, ["d1"])
        P.add("dve", lambda e: e.tensor_copy(out=d1[:, 1024:2048], in_=pooled2[:, 7, :]), PL2, ["d1"])
        dbg_out(d1[:, :], 128, 2048, ["d1"], 0)
        P.build()
        return nc, P, A
    for nme in ("pooled", "invc", "wu0", "wu1", "wu2", "ue0", "ue1", "sA", "sB", "wg", "xgh", "rsh"):
        A.release(nme)

    mixedl = [A.alloc(f"mixed{i}", [128, 8, NTOK], BF16) for i in range(2)]
    wgp = [A.alloc(f"wgp{i}", [128, 16, 128], BF16) for i in range(2)]
    wga = [A.alloc(f"wga{i}", [128, 16, 128], BF16) for i in range(2)]
    wpo = [A.alloc(f"wpo{i}", [128, 8, 128], BF16) for i in range(2)]
    wao = [A.alloc(f"wao{i}", [128, 16, 128], BF16) for i in range(2)]
    lg = [A.alloc(f"lg{i}", [128, 512], F32) for i in range(2)]
    gt = [A.alloc(f"gt{i}", [128, 512], F32) for i in range(2)]
    m1 = [A.alloc(f"m1{i}", [128, 512], F32) for i in range(2)]
    for n in range(16):
        i2 = n % 2
        def load_d2(nx):
            ix = nx % 2
            wdma(wgp[ix][:, :, :], wview(w_in, 2112 + nx * 128, 2112 + (nx + 1) * 128), f"wgp{ix}")
            wdma(wga[ix][:, :, :], wview(w_in, 2112 + D + nx * 128, 2112 + D + (nx + 1) * 128), f"wga{ix}")
            wdma(wpo[ix][:, :, :], wview(w_pool_o, nx * 128, (nx + 1) * 128), f"wpo{ix}")
            wdma(wao[ix][:, :, :], wview(w_attn_o, nx * 128, (nx + 1) * 128), f"wao{ix}")
        if n == 0:
            load_d2(0)
        if n + 1 < 16:
            load_d2(n + 1)
        for hh in range(2):
            T = slice(hh * 512, (hh + 1) * 512)
            bgp, bga, byp, bya = nb(), nb(), nb(), nb()
            for kc in range(16):
                mm(bgp, banks[bgp][:, :], wgp[i2][:, kc, :], xgo[:, kc, T], kc == 0, kc == 15, [f"wgp{i2}", ("xgo", hh)])
            for kc in range(16):
                mm(bga, banks[bga][:, :], wga[i2][:, kc, :], xgo[:, kc, T], kc == 0, kc == 15, [f"wga{i2}", ("xgo", hh)])
            for kc in range(8):
                mm(byp, banks[byp][:, :], wpo[i2][:, kc, :], pooled2[:, kc, T], kc == 0, kc == 7, [f"wpo{i2}", ("pooled2", kc)])
            for kc in range(16):
                mm(bya, banks[bya][:, :], wao[i2][:, kc, :], oall[:, kc, T], kc == 0, kc == 15, [f"wao{i2}", ("oall", kc)])
            for which, bg, by in ((0, bgp, byp), (1, bga, bya)):
                P.add("dve", lambda e, bg=bg, which=which, T=T: e.tensor_tensor(out=lg[which][:, :], in0=banks[bg][:, :], in1=rso[:, T], op=ALU.mult),
                      [PS(bg), ("rso", hh)], [f"lg{which}"])
                P.add("act", lambda e, which=which, n=n: e.activation(out=gt[which][:, :], in_=lg[which][:, :], func=AF.Sigmoid,
                                                                      bias=C("b_gate", which * 16 + n), scale=1.0),
                      [f"lg{which}", "cols"], [f"gt{which}"])
                P.add("dve", lambda e, by=by, which=which: e.tensor_tensor(out=m1[which][:, :], in0=banks[by][:, :], in1=gt[which][:, :], op=ALU.mult),
                      [PS(by), f"gt{which}"], [f"m1{which}"])
            P.add("dve", lambda e, n=n, T=T: e.tensor_tensor(out=mixedl[n // 8][:, n % 8, T], in0=m1[0][:, :], in1=m1[1][:, :], op=ALU.add),
                  ["m10", "m11"], [(f"mixed{n // 8}", n)])
    MIX = [(f"mixed{n // 8}", n) for n in range(16)]
    if stage == 5:
        d1 = A.alloc("d1", [128, 2048], F32)
        P.add("dve", lambda e: e.tensor_copy(out=d1[:, 0:1024], in_=mixedl[0][:, 0, :]), MIX, ["d1"])
        P.add("dve", lambda e: e.tensor_copy(out=d1[:, 1024:2048], in_=mixedl[1][:, 7, :]), MIX, ["d1"])
        dbg_out(d1[:, :], 128, 2048, ["d1"], 0)
        P.build()
        return nc, P, A
    for nme in ("oall", "pooled2", "xgo", "rso", "tmpa2", "lg0", "lg1", "gt0", "gt1", "m10", "m11") + tuple(
            f"{w}{i}" for w in ("wgp", "wga", "wpo", "wao") for i in range(2)):
        A.release(nme)

    h2 = A.alloc("h2", [128, 16, NTOK], BF16)
    rsf = A.alloc("rsf", [128, NTOK], F32)
    wo = [A.alloc(f"wo{i}", [128, 16, 128], BF16) for i in range(3)]
    xm = [A.alloc(f"xm{i}", [128, 512], F32) for i in range(4)]
    rot["set"] = [0, 1, 2, 3, 4, 5]
    SSQ = [6, 7]
    xi = 0
    pend = []
    for n in range(16):
        i2 = n % 3
        def load_wo(nx):
            wdma(wo[nx % 3][:, :, :], wview(w_out, nx * 128, (nx + 1) * 128), f"wo{nx % 3}")
        if n == 0:
            load_wo(0)
            load_wo(1)
        if n + 2 < 16:
            load_wo(n + 2)
        for hh in range(2):
            T = slice(hh * 512, (hh + 1) * 512)
            s = xi % 4
            it = n * 2 + hh
            if it == 0:
                for it2 in (0, 1):
                    ldma(xtd[it2 % 4][:, :], xT_own[(it2 // 2) * 128:(it2 // 2 + 1) * 128, (it2 % 2) * 512:(it2 % 2 + 1) * 512], f"xt{it2 % 4}")
            if it + 2 < 32:
                it2 = it + 2
                ldma(xtd[it2 % 4][:, :], xT_own[(it2 // 2) * 128:(it2 // 2 + 1) * 128, (it2 % 2) * 512:(it2 % 2 + 1) * 512], f"xt{it2 % 4}")
            xi += 1
            b = nb()
            for kc in range(16):
                mm(b, banks[b][:, :], wo[i2][:, kc, :], mixedl[kc // 8][:, kc % 8, T], kc == 0, kc == 15, [f"wo{i2}", (f"mixed{kc // 8}", kc)], kind=2)
            P.add("dve", lambda e, b=b, s=s: e.tensor_tensor(out=xm[s][:, :], in0=banks[b][:, :], in1=xtd[s][:, :], op=ALU.add),
                  [PS(b), f"xt{s}"], [f"xm{s}"])
            P.add("sp", lambda e, s=s, n=n, T=T: e.dma_start(out=xmid_d[n * 128:(n + 1) * 128, T], in_=xm[s][:, :]),
                  [f"xm{s}"], [("xmid", n, hh)], dma=True, semkey=("xmid_st", s))
            P.add("act", lambda e, s=s: e.activation(out=sqd[s][:, :], in_=xm[s][:, :], func=AF.Square),
                  [f"xm{s}"], [f"sq{s}"])
            pend.append((SSQ[hh], s, n))
            if len(pend) > 1:
                pb, ps_, pn = pend.pop(0)
                mm(pb, banks[pb][:, :], ones_bf[:, :], sqd[ps_][:, :], pn == 0, pn == 15, ["ones_bf", f"sq{ps_}"])
            P.add("act", lambda e, s=s, n=n, T=T: e.activation(out=h2[:, n, T], in_=xm[s][:, :], func=AF.Copy, scale=C("g_ffn", n)),
                  [f"xm{s}", "cols"], [("h2", hh)])
    while pend:
        pb, ps_, pn = pend.pop(0)
        mm(pb, banks[pb][:, :], ones_bf[:, :], sqd[ps_][:, :], pn == 0, pn == 15, ["ones_bf", f"sq{ps_}"])
    tmpf = A.alloc("tmpf", [128, 512], F32)
    for hh in range(2):
        rstd_from_psum(SSQ[hh], 512, 1.0 / D, "eps", rsf[:, hh * 512:(hh + 1) * 512], ("rsf", hh), tmpf, "tmpf")
    rot["set"] = list(range(8))
    if stage == 6:
        d1 = A.alloc("d1", [128, 2048], F32)
        P.add("dve", lambda e: e.tensor_copy(out=d1[:, 0:1024], in_=h2[:, 0, :]), [("h2", 0), ("h2", 1)], ["d1"])
        P.add("dve", lambda e: e.tensor_copy(out=d1[:, 1024:2048], in_=rsf[:, :]), [("rsf", 0), ("rsf", 1)], ["d1"])
        dbg_out(d1[:, :], 128, 2048, ["d1"], 0)
        P.build()
        return nc, P, A
    for nme in ("mixed0", "mixed1", "wo0", "wo1", "wo2", "tmpf"):
        A.release(nme)

    NF = DFF // 128
    actl = [A.alloc(f"actT{i}", [128, 11, NTOK], BF16) for i in range(4)]
    wfg = [A.alloc(f"wfg{i}", [128, 16, 128], BF16) for i in range(3)]
    wfu = [A.alloc(f"wfu{i}", [128, 16, 128], BF16) for i in range(3)]
    gts = [A.alloc(f"gts{i}", [128, 512], F32) for i in range(2)]
    uts = [A.alloc(f"uts{i}", [128, 512], F32) for i in range(2)]
    k = 0
    for f in range(NF):
        i2 = f % 3
        def load_f(fx):
            wdma(wfg[fx % 3][:, :, :], wview(w_gate, fx * 128, (fx + 1) * 128), f"wfg{fx % 3}")
            wdma(wfu[fx % 3][:, :, :], wview(w_up, fx * 128, (fx + 1) * 128), f"wfu{fx % 3}")
        if f == 0:
            load_f(0)
            load_f(1)
        if f + 2 < NF:
            load_f(f + 2)
        for hh in range(2):
            T = slice(hh * 512, (hh + 1) * 512)
            j2 = k % 2
            k += 1
            bg, bu = nb(), nb()
            for kc in range(16):
                mm(bg, banks[bg][:, :], wfg[i2][:, kc, :], h2[:, kc, T], kc == 0, kc == 15, [f"wfg{i2}", ("h2", hh)])
            for kc in range(16):
                mm(bu, banks[bu][:, :], wfu[i2][:, kc, :], h2[:, kc, T], kc == 0, kc == 15, [f"wfu{i2}", ("h2", hh)])
            P.add("dve", lambda e, bg=bg, j2=j2, T=T: e.tensor_tensor(out=gts[j2][:, :], in0=banks[bg][:, :], in1=rsf[:, T], op=ALU.mult),
                  [PS(bg), ("rsf", hh)], [f"gts{j2}"])
            P.add("act", lambda e, j2=j2: e.activation(out=gts[j2][:, :], in_=gts[j2][:, :], func=AF.Silu), [f"gts{j2}"], [f"gts{j2}"])
            P.add("dve", lambda e, bu=bu, j2=j2, T=T: e.tensor_tensor(out=uts[j2][:, :], in0=banks[bu][:, :], in1=rsf[:, T], op=ALU.mult),
                  [PS(bu), ("rsf", hh)], [f"uts{j2}"])
            P.add("dve", lambda e, j2=j2, f=f, T=T: e.tensor_tensor(out=actl[f // 11][:, f % 11, T], in0=gts[j2][:, :], in1=uts[j2][:, :], op=ALU.mult),
                  [f"gts{j2}", f"uts{j2}"], [(f"actT{f // 11}", f, hh)])
    for nme in ("h2", "wfg0", "wfg1", "wfg2", "wfu0", "wfu1", "wfu2", "rsf"):
        A.release(nme)
    wd = [A.alloc(f"wd{i}", [128, NF, 128], BF16) for i in range(2)]
    for n in range(16):
        i2 = n % 2
        def load_wd(nx):
            wdma(wd[nx % 2][:, :, :], wview(w_down, nx * 128, (nx + 1) * 128), f"wd{nx % 2}")
        if n == 0:
            load_wd(0)
        if n + 1 < 16:
            load_wd(n + 1)
        for hh in range(2):
            T = slice(hh * 512, (hh + 1) * 512)
            s = xi % 4
            it = n * 2 + hh

            def load_xm(itx):
                sx = (xi0 + itx) % 4
                nx, hx = itx // 2, itx % 2
                P.add("sp", lambda e: e.dma_start(out=xtd[sx][:, :], in_=xmid_d[nx * 128:(nx + 1) * 128, hx * 512:(hx + 1) * 512]),
                      [("xmid", nx, hx)], [f"xt{sx}"], dma=True)
            if it == 0:
                xi0 = xi
                load_xm(0)
                load_xm(1)
            if it + 2 < 32:
                load_xm(it + 2)
            xi += 1
            b = nb()
            for kc in range(NF):
                mm(b, banks[b][:, :], wd[i2][:, kc, :], actl[kc // 11][:, kc % 11, T], kc == 0, kc == NF - 1, [f"wd{i2}", (f"actT{kc // 11}", kc, hh)], kind=2)
            P.add("dve", lambda e, b=b, s=s: e.tensor_tensor(out=xm[s][:, :], in0=banks[b][:, :], in1=xtd[s][:, :], op=ALU.add),
                  [PS(b), f"xt{s}"], [f"xm{s}"])
            P.add("sp", lambda e, s=s, n=n, T=T: e.dma_start(out=outT[n * 128:(n + 1) * 128, T], in_=xm[s][:, :]),
                  [f"xm{s}"], [("OUT", n, hh)], dma=True, semkey=("out_st", s))
    P.build()
    return nc, P, A


def own_token_index(j):
    return np.concatenate([np.arange(128 * (4 * t + j), 128 * (4 * t + j) + 128) for t in range(8)])


def halo_token_index(j):
    return np.concatenate([np.arange(128 * (4 * t + j) - 16, 128 * (4 * t + j)) for t in range(8)])


def make_in_maps(inp):
    f32 = lambda a: np.ascontiguousarray(np.asarray(a, dtype=np.float32))
    x = np.asarray(inp["x"], dtype=np.float32)
    positions = np.asarray(inp["positions"]).astype(np.int32)
    w_in = f32(inp["w_in"][0])
    kr0 = 1024
    w_kr = np.ascontiguousarray(np.concatenate([w_in[:, kr0:kr0 + 64], w_in[:, kr0 + 32:kr0 + 64], w_in[:, kr0:kr0 + 32]], axis=1))
    wqb = f32(inp["w_q_b"][0]).reshape(512, NH, 192)
    w_qb2 = np.ascontiguousarray(np.concatenate([wqb, wqb[:, :, 160:192], wqb[:, :, 128:160]], axis=2).reshape(512, NH * 256))
    shared = {
        "w_in": w_in, "w_kr": w_kr, "w_qb2": w_qb2,
        "w_kn": np.ascontiguousarray(f32(inp["w_kv_b"][0]).reshape(512, NH, 256)[:, :, :128].reshape(512, NH * 128)),
        "w_v": np.ascontiguousarray(f32(inp["w_kv_b"][0]).reshape(512, NH, 256)[:, :, 128:].reshape(512, NH * 128)),
        "w_attn_o": f32(inp["w_attn_o"][0]),
        "w_pool_grp": f32(inp["w_pool_grp"][0]).reshape(1024, 256),
        "w_pool_o": f32(inp["w_pool_o"][0]),
        "w_out": f32(inp["w_out"][0]),
        "w_ffn_gate": f32(inp["w_ffn_gate"][0]),
        "w_ffn_up": f32(inp["w_ffn_up"][0]),
        "w_ffn_down": f32(inp["w_ffn_down"][0]),
    }
    cols = np.zeros((128, CL["_n"]), np.float32)

    def putc(name, arr):
        a = np.asarray(arr, np.float32).reshape(-1, 128).T
        cols[:, CL[name]:CL[name] + a.shape[1]] = a

    putc("g_attn", inp["attn_norm_g"][0])
    putc("g_ffn", inp["ffn_norm_g"][0])
    putc("g_qa", inp["q_a_norm_g"][0])
    putc("g_kva", inp["kv_a_norm_g"][0])
    putc("b_gate", inp["b_gate"][0])
    putc("pool_scale", np.asarray(inp["pool_scale"][0]).reshape(-1))
    gq = np.asarray(inp["q_norm_g"][0], np.float32)
    gk = np.asarray(inp["k_norm_g"][0], np.float32)
    for nm, g in (("gq", gq), ("gk", gk)):
        cols[:, CL[nm + "_n"]] = g[:128]
        cols[:64, CL[nm + "_r"]] = g[128:192]
        cols[:64, CL[nm + "_rs"]] = np.concatenate([g[160:192], g[128:160]])
    half = 32
    inv_freq = (10000.0 ** (-np.arange(half, dtype=np.float32) / half)).astype(np.float32)
    cols[:64, CL["invfreq"]] = np.concatenate([inv_freq, inv_freq])
    cols[:64, CL["nsign"]] = np.concatenate([-np.ones(32), np.ones(32)])
    for nm_ in ("gq_r", "gq_rs", "gk_r", "gk_rs", "invfreq", "nsign"):
        cols[64:128, CL[nm_]] = cols[0:64, CL[nm_]]
    cols[:, CL["eps"]] = EPS
    cols[:, CL["eps192"]] = 192 * EPS
    cols[:, CL["negpi"]] = -math.pi
    in_maps = []
    wins = (2, 4, 8, 16)
    for c in range(8):
        b, j = c // 4, c % 4
        own = own_token_index(j)
        halo = halo_token_index(j)
        xb = x[b]
        xT_all = np.ascontiguousarray(xb.T)
        xT_own = np.ascontiguousarray(xb[own].T)
        xh = np.where((halo >= 0)[:, None], xb[np.maximum(halo, 0)], 0.0).astype(np.float32)
        xT_halo = np.ascontiguousarray(xh.T)
        pos_all = np.ascontiguousarray(np.broadcast_to(positions[b][None, :], (64, SEQ)))
        pos_own = np.ascontiguousarray(np.broadcast_to(positions[b][own][None, :], (128, NTOK)))
        kk = np.arange(128)[:, None]
        qq = np.arange(128)[None, :]
        masks = np.concatenate([((128 * (cp - j) + kk) <= qq).astype(np.float32) for cp in range(4)], axis=1)
        invcnt = np.stack([1.0 / np.minimum(w, own + 1).astype(np.float32) for w in wins], axis=0).reshape(1, 4 * NTOK)
        invcnt = np.ascontiguousarray(np.broadcast_to(invcnt, (128, 4 * NTOK))).astype(np.float32)
        m = dict(shared)
        m.update({"xT_all": xT_all, "xT_own": xT_own, "xT_halo": xT_halo, "pos_all": pos_all, "pos_own": pos_own,
                  "cols": cols, "masks": np.ascontiguousarray(masks), "invcnt": invcnt})
        in_maps.append(m)
    return in_maps


_CACHE = {}


def kernel(**inputs):
    in_maps = make_in_maps(inputs)
    if "nc" not in _CACHE:
        _CACHE["nc"] = build_program()[0]
    nc = _CACHE["nc"]
    res = run_bass_kernel_spmd(nc, in_maps, core_ids=list(range(8)))
    out = np.zeros((2, SEQ, D), np.float32)
    for c in range(8):
        b, j = c // 4, c % 4
        out[b, own_token_index(j), :] = res.results[c]["outT"].T
    return out
```
